# Optimizing a Trainium2 kernel written in Bass

```python
import math
import jax
import jax.numpy as jnp
from jax import lax
import numpy as np

D_MODEL = 2048
BATCH = 16
SEQ = 256
DEPTH = 4
DEC_BATCH = 4
DEC_SEQ = 2048
PAST_LEN = 512

GRID_W = 64
NA_HEADS = 8
NA_HD = 128
NA_ROWS = 8
NA_COLS = 16
GQA_HEADS = 8
GQA_KV = 2
GQA_HD = 128
ROPE_BASE = 10000.0
SSD_INNER = 1024
SSD_HEADDIM = 64
SSD_HEADS = SSD_INNER // SSD_HEADDIM
SSD_STATE = 128
SSD_GROUPS = 2
SSD_CONV = 3
SSD_CHUNK = 128
SSD_CONV_CH = SSD_INNER + 2 * SSD_GROUPS * SSD_STATE
BRANCH_W = 1024
N_IN = 3 * NA_HEADS * NA_HD + (GQA_HEADS + 2 * GQA_KV) * GQA_HD + SSD_INNER + SSD_CONV_CH + 2 * SSD_HEADS + 3 * D_MODEL
FFN_HIDDEN = -(-8 * D_MODEL // (3 * 256)) * 256
Q_BLOCK = 128
EPS = 1e-6
NEG = -1e30

kernel_name = "hybrid_dit_prefix_step"


def rms_norm(x, w):
    xf = x.astype(jnp.float32)
    xf = xf * lax.rsqrt(jnp.mean(xf * xf, axis=-1, keepdims=True) + EPS)
    return (xf * w.astype(jnp.float32)).astype(x.dtype)


def proj_split_points():
    sizes = [NA_HEADS * NA_HD] * 3 + [GQA_HEADS * GQA_HD, GQA_KV * GQA_HD, GQA_KV * GQA_HD,
                                      SSD_INNER, SSD_CONV_CH, 2 * SSD_HEADS]
    return [int(v) for v in np.cumsum(sizes)]


def axial_rope(x):
    L, d = x.shape[1], x.shape[-1]
    half = d // 2
    nf = half // 2
    t = jnp.arange(L)
    inv = ROPE_BASE ** (-jnp.arange(nf, dtype=jnp.float32) / nf)

    def rot(xp, pos):
        ang = pos.astype(jnp.float32)[:, None] * inv[None, :]
        cos = jnp.cos(ang)[None, :, None, :]
        sin = jnp.sin(ang)[None, :, None, :]
        x1 = xp[..., :nf].astype(jnp.float32)
        x2 = xp[..., nf:].astype(jnp.float32)
        return jnp.concatenate([x1 * cos - x2 * sin, x2 * cos + x1 * sin], axis=-1)

    out = jnp.concatenate([rot(x[..., :half], t // GRID_W), rot(x[..., half:], t % GRID_W)], axis=-1)
    return out.astype(x.dtype)


def blocked_attention(q, k, v, n_kv):
    b, L, H, d = q.shape
    g = H // n_kv
    nb = L // Q_BLOCK
    scale = d ** -0.5
    qb = q.reshape(b, nb, Q_BLOCK, n_kv, g, d).transpose(1, 0, 2, 3, 4, 5)

    def one_block(qi):
        s = jnp.einsum("bqkgd,bskd->bkgqs", qi, k).astype(jnp.float32) * scale
        pr = jax.nn.softmax(s, axis=-1).astype(v.dtype)
        return jnp.einsum("bkgqs,bskd->bqkgd", pr, v)

    o = lax.map(one_block, qb)
    return o.transpose(1, 0, 2, 3, 4, 5).reshape(b, L, H, d)


def na_latent(q, k, v, ck, cv, rpb):
    b, L, H, d = q.shape
    R = L // GRID_W
    WR = min(NA_ROWS, R)
    scale = d ** -0.5
    qg = q.reshape(b, R, GRID_W, H, d)
    kg = k.reshape(b, R, GRID_W, H, d)
    vg = v.reshape(b, R, GRID_W, H, d)
    r = jnp.arange(R)
    rs = jnp.clip(r - WR // 2, 0, R - WR)
    row_idx = rs[:, None] + jnp.arange(WR)[None, :]
    kw = kg[:, row_idx]
    vw = vg[:, row_idx]
    cidx = jnp.arange(GRID_W)
    cs = jnp.clip(cidx - NA_COLS // 2, 0, GRID_W - NA_COLS)
    cmask = (cidx[None, :] >= cs[:, None]) & (cidx[None, :] < cs[:, None] + NA_COLS)
    dr_idx = row_idx - r[:, None] + NA_ROWS - 1
    dc_idx = jnp.clip(cidx[None, :] - cidx[:, None] + NA_COLS - 1, 0, 2 * NA_COLS - 2)
    bias = rpb[:, dr_idx[:, :, None, None], dc_idx[None, None, :, :]]
    bias = bias.transpose(1, 0, 3, 2, 4).astype(jnp.float32)
    s_loc = jnp.einsum("brqhd,briwhd->brhqiw", qg, kw).astype(jnp.float32) * scale + bias[None]
    s_loc = jnp.where(cmask[:, None, :], s_loc, NEG).reshape(b, R, H, GRID_W, WR * GRID_W)
    s_ctx = jnp.einsum("brqhd,bshd->brhqs", qg, ck).astype(jnp.float32) * scale
    pr = jax.nn.softmax(jnp.concatenate([s_loc, s_ctx], axis=-1), axis=-1).astype(v.dtype)
    p_loc = pr[..., :WR * GRID_W].reshape(b, R, H, GRID_W, WR, GRID_W)
    p_ctx = pr[..., WR * GRID_W:]
    o = jnp.einsum("brhqiw,briwhd->brqhd", p_loc, vw) + jnp.einsum("brhqs,bshd->brqhd", p_ctx, cv)
    return o.reshape(b, L, H, d)


def ssd_scan(x, dt, A, B, C, h0):
    b, L, H, P = x.shape
    G, N = B.shape[2], B.shape[3]
    nc = L // SSD_CHUNK
    rep = H // G
    f32 = jnp.float32
    x = x.astype(f32).reshape(b, nc, SSD_CHUNK, H, P)
    dt = dt.astype(f32).reshape(b, nc, SSD_CHUNK, H)
    B = jnp.repeat(B.astype(f32), rep, axis=2).reshape(b, nc, SSD_CHUNK, H, N)
    C = jnp.repeat(C.astype(f32), rep, axis=2).reshape(b, nc, SSD_CHUNK, H, N)
    a_cum = jnp.cumsum(dt * A, axis=2)
    seg = a_cum[:, :, :, None, :] - a_cum[:, :, None, :, :]
    tril = jnp.tril(jnp.ones((SSD_CHUNK, SSD_CHUNK), dtype=bool))
    lmat = jnp.exp(jnp.where(tril[:, :, None], seg, -jnp.inf))
    scores = jnp.einsum("bclhn,bcshn->bclsh", C, B) * lmat * dt[:, :, None, :, :]
    y_diag = jnp.einsum("bclsh,bcshp->bclhp", scores, x)
    decay_end = jnp.exp(a_cum[:, :, -1:, :] - a_cum)
    states = jnp.einsum("bcshn,bcsh,bcshp->bchpn", B, decay_end * dt, x)
    chunk_decay = jnp.exp(a_cum[:, :, -1, :])

    def step(h, inp):
        s_c, d_c = inp
        return d_c[:, :, None, None] * h + s_c, h

    h_final, h_starts = lax.scan(step, h0.astype(f32),
                                 (states.transpose(1, 0, 2, 3, 4), chunk_decay.transpose(1, 0, 2)))
    h_starts = h_starts.transpose(1, 0, 2, 3, 4)
    y_off = jnp.einsum("bclhn,bchpn,bclh->bclhp", C, h_starts, jnp.exp(a_cum))
    return (y_diag + y_off).reshape(b, L, H, P), h_final


def ssd_mixer(z, xbc, dt_raw, conv_w, conv_b, dt_bias, a_log, d_skip, norm_w, h0f, h0b):
    b, L, _ = z.shape
    xbc = lax.conv_general_dilated(xbc, conv_w[:, None, :], (1,), [(SSD_CONV // 2, SSD_CONV // 2)],
                                   dimension_numbers=("NWC", "WIO", "NWC"),
                                   feature_group_count=SSD_CONV_CH) + conv_b
    xbc = jax.nn.silu(xbc)
    xs, bm, cm = jnp.split(xbc, [SSD_INNER, SSD_INNER + SSD_GROUPS * SSD_STATE], axis=-1)
    xs = xs.reshape(b, L, SSD_HEADS, SSD_HEADDIM)
    bm = bm.reshape(b, L, SSD_GROUPS, SSD_STATE)
    cm = cm.reshape(b, L, SSD_GROUPS, SSD_STATE)
    dt = jax.nn.softplus(dt_raw.reshape(b, L, 2, SSD_HEADS).astype(jnp.float32) + dt_bias.astype(jnp.float32))
    A = -jnp.exp(a_log.astype(jnp.float32))
    y_f, h_f = ssd_scan(xs, dt[:, :, 0], A[0], bm, cm, h0f)
    y_b, h_b = ssd_scan(xs[:, ::-1], dt[:, ::-1, 1], A[1], bm[:, ::-1], cm[:, ::-1], h0b)
    d_tot = (d_skip[0] + d_skip[1]).astype(jnp.float32)[:, None]
    y = y_f + y_b[:, ::-1] + d_tot * xs.astype(jnp.float32)
    y = y.reshape(b, L, SSD_INNER).astype(z.dtype)
    y = rms_norm(y * jax.nn.silu(z), norm_w)
    return y, h_f, h_b


def trunk_layer(x, cvec, p, ctx):
    b, L, _ = x.shape
    mods = jax.nn.silu(cvec) @ p["ada_w"] + p["ada_b"]
    sh1, sc1, g1, sh2, sc2, g2 = [m[:, None, :] for m in jnp.split(mods, 6, axis=-1)]
    h = rms_norm(x, p["n_mix_pre"]) * (1 + sc1) + sh1
    u = h @ p["w_in"]
    qa, ka, va, qc, kc, vc, z, xbc, dt_raw, gl = jnp.split(u, proj_split_points(), axis=-1)
    qa = qa.reshape(b, L, NA_HEADS, NA_HD)
    ka = ka.reshape(b, L, NA_HEADS, NA_HD)
    va = va.reshape(b, L, NA_HEADS, NA_HD)
    qc = rms_norm(qc.reshape(b, L, GQA_HEADS, GQA_HD), p["q_norm"])
    kc = rms_norm(kc.reshape(b, L, GQA_KV, GQA_HD), p["k_norm"])
    vc = vc.reshape(b, L, GQA_KV, GQA_HD)
    if ctx is None:
        oa = blocked_attention(qa, ka, va, NA_HEADS)
        oc = blocked_attention(qc, kc, vc, GQA_KV)
        h0 = jnp.zeros((b, SSD_HEADS, SSD_HEADDIM, SSD_STATE), jnp.float32)
        ob, h_f, h_b = ssd_mixer(z, xbc, dt_raw, p["conv_w"], p["conv_b"], p["dt_bias"], p["a_log"],
                                 p["d_skip"], p["ssd_norm_w"], h0, h0)
        new = (ka, va, kc, vc, jnp.stack([h_f, h_b], axis=1).astype(x.dtype))
    else:
        ck_a, cv_a, ck_c, cv_c, st = ctx
        oa = na_latent(qa, ka, va, ck_a, cv_a, p["rpb"])
        qc = axial_rope(qc)
        kc = axial_rope(kc)
        oc = blocked_attention(qc, jnp.concatenate([kc, ck_c], axis=1),
                               jnp.concatenate([vc, cv_c], axis=1), GQA_KV)
        ob, _, _ = ssd_mixer(z, xbc, dt_raw, p["conv_w"], p["conv_b"], p["dt_bias"], p["a_log"],
                             p["d_skip"], p["ssd_norm_w"], st[:, 0], st[:, 1])
        new = None
    pa = oa.reshape(b, L, BRANCH_W) @ p["w_branch"][0]
    pb = ob @ p["w_branch"][1]
    pc = oc.reshape(b, L, BRANCH_W) @ p["w_branch"][2]
    ga, gb, gc = jnp.split(jax.nn.sigmoid(gl + p["gate_b"]), 3, axis=-1)
    mix = (ga * pa + gb * pb + gc * pc) @ p["w_out"]
    x = x + g1 * rms_norm(mix, p["n_mix_post"])
    h = rms_norm(x, p["n_ffn_pre"]) * (1 + sc2) + sh2
    gate, up = jnp.split(h @ p["ffn_w_up"], 2, axis=-1)
    f = (jax.nn.silu(gate) * up) @ p["ffn_w_down"]
    x = x + g2 * rms_norm(f, p["n_ffn_post"])
    return x, new


def setup_inputs(seed: int = 0) -> dict:
    key = jax.random.key(seed)
    ks = iter(jax.random.split(key, 40))
    f32 = jnp.float32
    D = D_MODEL

    def nrm(shape, scale):
        return scale * jax.random.normal(next(ks), shape, f32)

    def gain(shape):
        return 1.0 + nrm(shape, 0.02)

    dt0 = jnp.exp(jax.random.uniform(next(ks), (DEPTH, 2, SSD_HEADS), f32,
                                     minval=math.log(1e-3), maxval=math.log(1e-1)))
    dt_bias = dt0 + jnp.log(-jnp.expm1(-dt0))
    a_log = jnp.log(jax.random.uniform(next(ks), (DEPTH, 2, SSD_HEADS), f32, minval=1.0, maxval=16.0))
    return {
        "x_prompt": nrm((BATCH, SEQ, D), 1.0),
        "x_sample": nrm((DEC_BATCH, DEC_SEQ, D), 1.0),
        "cache_na_k": nrm((DEC_BATCH, DEPTH, PAST_LEN, NA_HEADS, NA_HD), 1.0),
        "cache_na_v": nrm((DEC_BATCH, DEPTH, PAST_LEN, NA_HEADS, NA_HD), 1.0),
        "cache_gqa_k": nrm((DEC_BATCH, DEPTH, PAST_LEN, GQA_KV, GQA_HD), 1.0),
        "cache_gqa_v": nrm((DEC_BATCH, DEPTH, PAST_LEN, GQA_KV, GQA_HD), 1.0),
        "state_ssd": nrm((DEC_BATCH, DEPTH, 2, SSD_HEADS, SSD_HEADDIM, SSD_STATE), 0.1),
        "c": nrm((DEC_BATCH, D), 1.0),
        "c_ctx": nrm((D,), 1.0),
        "ada_w": nrm((DEPTH, D, 6 * D), 0.5 * D ** -0.5),
        "ada_b": nrm((DEPTH, 6 * D), 0.02),
        "norm_mix_pre": gain((DEPTH, D)),
        "norm_mix_post": gain((DEPTH, D)),
        "norm_ffn_pre": gain((DEPTH, D)),
        "norm_ffn_post": gain((DEPTH, D)),
        "w_in": nrm((DEPTH, D, N_IN), D ** -0.5),
        "gate_b": nrm((DEPTH, 3 * D), 0.02),
        "na_rpb": nrm((DEPTH, NA_HEADS, 2 * NA_ROWS - 1, 2 * NA_COLS - 1), 0.1),
        "gqa_q_norm": gain((DEPTH, GQA_HD)),
        "gqa_k_norm": gain((DEPTH, GQA_HD)),
        "ssd_conv_w": nrm((DEPTH, SSD_CONV, SSD_CONV_CH), SSD_CONV ** -0.5),
        "ssd_conv_b": nrm((DEPTH, SSD_CONV_CH), 0.02),
        "ssd_dt_bias": dt_bias,
        "ssd_a_log": a_log,
        "ssd_d": 0.5 + nrm((DEPTH, 2, SSD_HEADS), 0.05),
        "ssd_norm_w": gain((DEPTH, SSD_INNER)),
        "w_branch": nrm((DEPTH, 3, BRANCH_W, D), BRANCH_W ** -0.5),
        "w_out": nrm((DEPTH, D, D), D ** -0.5),
        "ffn_w_up": nrm((DEPTH, D, 2 * FFN_HIDDEN), D ** -0.5),
        "ffn_w_down": nrm((DEPTH, FFN_HIDDEN, D), FFN_HIDDEN ** -0.5),
    }


def reference(x_prompt, x_sample, cache_na_k, cache_na_v, cache_gqa_k, cache_gqa_v, state_ssd,
              c, c_ctx, ada_w, ada_b, norm_mix_pre, norm_mix_post, norm_ffn_pre, norm_ffn_post,
              w_in, gate_b, na_rpb, gqa_q_norm, gqa_k_norm, ssd_conv_w, ssd_conv_b, ssd_dt_bias,
              ssd_a_log, ssd_d, ssd_norm_w, w_branch, w_out, ffn_w_up, ffn_w_down):
    xp = x_prompt
    xs = x_sample
    na_k, na_v, gqa_k, gqa_v, ssd_st = [], [], [], [], []
    for l in range(DEPTH):
        p = {
            "ada_w": ada_w[l], "ada_b": ada_b[l],
            "n_mix_pre": norm_mix_pre[l], "n_mix_post": norm_mix_post[l],
            "n_ffn_pre": norm_ffn_pre[l], "n_ffn_post": norm_ffn_post[l],
            "w_in": w_in[l], "gate_b": gate_b[l], "rpb": na_rpb[l],
            "q_norm": gqa_q_norm[l], "k_norm": gqa_k_norm[l],
            "conv_w": ssd_conv_w[l], "conv_b": ssd_conv_b[l], "dt_bias": ssd_dt_bias[l],
            "a_log": ssd_a_log[l], "d_skip": ssd_d[l], "ssd_norm_w": ssd_norm_w[l],
            "w_branch": w_branch[l], "w_out": w_out[l],
            "ffn_w_up": ffn_w_up[l], "ffn_w_down": ffn_w_down[l],
        }
        xp, (ka, va, kc, vc, st) = trunk_layer(xp, c_ctx[None, :], p, None)
        na_k.append(ka)
        na_v.append(va)
        gqa_k.append(kc)
        gqa_v.append(vc)
        ssd_st.append(st)
        ctx = (cache_na_k[:, l], cache_na_v[:, l], cache_gqa_k[:, l], cache_gqa_v[:, l], state_ssd[:, l])
        xs, _ = trunk_layer(xs, c, p, ctx)
    new_na_k = jnp.stack(na_k, axis=1)
    new_na_v = jnp.stack(na_v, axis=1)
    new_gqa_k = jnp.stack(gqa_k, axis=1)
    new_gqa_v = jnp.stack(gqa_v, axis=1)
    new_ssd = jnp.stack(ssd_st, axis=1)
    return (xp, xs, new_na_k, new_na_v, new_gqa_k, new_gqa_v, new_ssd)
```

```python
import math
from contextlib import ExitStack
import numpy as np
import concourse.bass as bass
import concourse.mybir as mybir
from concourse.bass_utils import run_bass_kernel_spmd
from concourse.bass_types import AP

F32 = mybir.dt.float32
BF16 = mybir.dt.bfloat16
AF = mybir.ActivationFunctionType
ALU = mybir.AluOpType
AX = mybir.AxisListType

D = 2048
DEPTH = 4
T = 2560
NT = 20
PT = 512
SEQS = [(0, 256, 0), (256, 256, 0), (512, 2048, 1)]
N_IN = 13344
FFN_H = 5632
EPS = 1e-6
NEG = -1e30


class Buf:
    __slots__ = ("name", "w", "r")

    def __init__(self, name):
        self.name = name
        self.w = {}
        self.r = {}


class Eng:
    def __init__(self, name, sem):
        self.name = name
        self.sem = sem
        self.count = 0
        self.ops = []
        self.seen = {}


class Fw:
    NDMA = 56

    def __init__(self, nc, stack):
        self.nc = nc
        self.stack = stack
        self.sems = []
        self.eng = {}
        for n in ("pe", "act", "dve", "pool", "sp"):
            s = stack.enter_context(nc.semaphore("prog_" + n))
            self.sems.append(s)
            self.eng[n] = Eng(n, len(self.sems) - 1)
        self.dma_slots = []
        for i in range(self.NDMA):
            s = stack.enter_context(nc.semaphore("dma_%d" % i))
            self.sems.append(s)
            self.dma_slots.append([len(self.sems) - 1, 0])
        self.dma_rr = 0
        self.nbuf = 0

    def buf(self, name=None):
        self.nbuf += 1
        return Buf(name or "b%d" % self.nbuf)

    def sb(self, name, shape, dt):
        self.nalloc = getattr(self, "nalloc", 0) + 1
        return self.stack.enter_context(self.nc.sbuf_tensor("%s_%d" % (name, self.nalloc), list(shape), dt))

    def ps(self, name, shape, dt=F32):
        self.nalloc = getattr(self, "nalloc", 0) + 1
        return self.stack.enter_context(self.nc.psum_tensor("%s_%d" % (name, self.nalloc), list(shape), dt))

    def _waits(self, e, reads, writes):
        need = {}
        for b in reads:
            for s, v in b.w.items():
                if need.get(s, 0) < v:
                    need[s] = v
        for b in writes:
            for s, v in b.w.items():
                if need.get(s, 0) < v:
                    need[s] = v
            for s, v in b.r.items():
                if need.get(s, 0) < v:
                    need[s] = v
        out = []
        for s, v in need.items():
            if e.seen.get(s, 0) < v:
                e.seen[s] = v
                out.append((s, v))
        return out

    def op(self, en, fn, reads=(), writes=()):
        e = self.eng[en]
        waits = self._waits(e, reads, writes)
        if en == "pe":
            waits = [(s, v) for (s, v) in waits if s != e.sem]
        e.count += 1
        ev = (e.sem, e.count)
        for b in reads:
            if b.r.get(ev[0], 0) < ev[1]:
                b.r[ev[0]] = ev[1]
        for b in writes:
            b.w = {ev[0]: ev[1]}
            b.r = {}
        e.ops.append((waits, fn, (e.sem, 1)))
        return ev

    def dma(self, fn, reads=(), writes=(), q="sp"):
        e = self.eng[q]
        slot = self.dma_slots[self.dma_rr]
        self.dma_rr = (self.dma_rr + 1) % self.NDMA
        waits = self._waits(e, reads, writes)
        if slot[1] > 0 and e.seen.get(slot[0], 0) < slot[1]:
            e.seen[slot[0]] = slot[1]
            waits.append((slot[0], slot[1]))
        slot[1] += 16
        ev = (slot[0], slot[1])
        for b in reads:
            if b.r.get(ev[0], 0) < ev[1]:
                b.r[ev[0]] = ev[1]
        for b in writes:
            b.w = {ev[0]: ev[1]}
            b.r = {}
        e.ops.append((waits, fn, (slot[0], 16)))
        return ev

    def final_wait(self, bufs):
        e = self.eng["sp"]
        waits = self._waits(e, bufs, bufs)
        e.ops.append((waits, None, None))

    def phase_end(self):
        e = self.eng["sp"]
        waits = []
        for sl in self.dma_slots:
            if sl[1] > 0 and e.seen.get(sl[0], 0) < sl[1]:
                e.seen[sl[0]] = sl[1]
                waits.append((sl[0], sl[1]))
        e.ops.append((waits, None, None))

    def emit(self):
        nc = self.nc
        handles = {"pe": "tensor", "act": "scalar", "dve": "vector", "pool": "gpsimd", "sp": "sync"}
        sems = self.sems
        with nc.Block() as block:
            for n, e in self.eng.items():
                if not e.ops:
                    continue

                def body(h, e=e):
                    for (waits, fn, inc) in e.ops:
                        for (s, v) in waits:
                            h.wait_ge(sems[s], v)
                        if fn is not None:
                            fn(h).then_inc(sems[inc[0]], inc[1])
                getattr(block, handles[n])(body)
        for e in self.eng.values():
            e.ops = []


class Ring:
    def __init__(self, fw, name, shape, dt, n, psum=False):
        self.items = []
        for i in range(n):
            t = (fw.ps if psum else fw.sb)("%s%d" % (name, i), shape, dt)
            self.items.append((t, fw.buf("%s%d" % (name, i))))
        self.i = 0

    def next(self):
        it = self.items[self.i]
        self.i = (self.i + 1) % len(self.items)
        return it


def host_consts():
    c = {}
    c["ident"] = np.eye(128, dtype=np.float32)
    c["ones"] = np.ones((128, 128), dtype=np.float32)
    k = np.arange(128)[:, None]
    j = np.arange(128)[None, :]
    c["ssdm"] = np.stack([
        (k > j), (k <= j), (k <= j),
        (k < j), (k >= j), (k >= j),
    ]).astype(np.float32)
    nf = 32
    inv = 10000.0 ** (-np.arange(nf, dtype=np.float64) / nf)
    t = np.arange(2048)
    d = np.arange(128)
    pos = np.where(d[:, None] < 64, (t // 64)[None, :], (t % 64)[None, :]).astype(np.float64)
    fr = inv[(d % 64) % 32][:, None]
    ang = (pos.astype(np.float32) * fr.astype(np.float32)).astype(np.float32)
    c["ropec"] = np.cos(ang).astype(np.float32)
    c["ropes"] = np.sin(ang).astype(np.float32)
    R = np.zeros((128, 128), dtype=np.float32)
    for m in range(128):
        if (m % 64) < 32:
            R[m + 32, m] = -1.0
        else:
            R[m - 32, m] = 1.0
    c["ropeR"] = R
    J = np.zeros((128, 128), dtype=np.float32)
    for m in range(128):
        J[(m // 64) * 64 + 63 - (m % 64), m] = 1.0
    c["flipJ"] = J
    masks = np.zeros((5, 128, 640), dtype=np.float32)
    a0s = []
    for ti, jt in enumerate([0, 1, 5, 14, 15]):
        tw = min(max(jt - 2, 0), 11)
        for q in range(128):
            rq = 2 * jt + q // 64
            cq = q % 64
            rs = min(max(rq - 4, 0), 24)
            cs = min(max(cq - 8, 0), 48)
            for rl in range(10):
                rk = 2 * tw + rl
                okr = (rk >= rs) and (rk < rs + 8)
                for ck in range(64):
                    ok = okr and (ck >= cs) and (ck < cs + 16)
                    masks[ti, q, rl * 64 + ck] = 0.0 if ok else NEG
    c["namask"] = masks
    return c


def na_tile_type(jt):
    if jt == 0:
        return 0
    if jt == 1:
        return 1
    if jt == 14:
        return 3
    if jt == 15:
        return 4
    return 2


def build_program(depth=DEPTH, debug=False, stop_after=None):
    nc = bass.Bass("TRN2", target_bir_lowering=False)

    def din(name, shape, dt=F32):
        return nc.dram_tensor(name, list(shape), dt, kind="ExternalInput").ap()

    def dout(name, shape, dt=F32):
        return nc.dram_tensor(name, list(shape), dt, kind="ExternalOutput").ap()

    def dscr(name, shape, dt=F32):
        return nc.dram_tensor(name, list(shape), dt, kind=("ExternalOutput" if debug else "Internal")).ap()

    L_ = depth
    xin = din("xin", [T, D])
    cv = din("cv", [2, D])
    c_nak = din("c_nak", [L_, 512, 8, 128])
    c_nav = din("c_nav", [L_, 512, 8, 128])
    c_gk = din("c_gk", [L_, 512, 2, 128])
    c_gv = din("c_gv", [L_, 512, 2, 128])
    c_ssd = din("c_ssd", [L_, 2, 1024, 128])
    ada_w = din("ada_w", [L_, D, 6 * D])
    ada_b = din("ada_b", [L_, 6 * D])
    n_w = [din(n, [L_, D]) for n in ("norm_mix_pre", "norm_mix_post", "norm_ffn_pre", "norm_ffn_post")]
    w_in = din("w_in", [L_, D, N_IN])
    gate_b = din("gate_b", [L_, 3 * D])
    na_rpb = din("na_rpb", [L_, 8 * 15 * 31])
    qn_w = din("gqa_q_norm", [L_, 128])
    kn_w = din("gqa_k_norm", [L_, 128])
    conv_w = din("ssd_conv_w", [L_, 3, 1536])
    conv_b = din("ssd_conv_b", [L_, 1536])
    dt_bias = din("ssd_dt_bias", [L_, 32])
    a_log = din("ssd_a_log", [L_, 32])
    ssd_d = din("ssd_d", [L_, 32])
    ssd_nw = din("ssd_norm_w", [L_, 1024])
    w_br = din("w_branch", [L_, 3, 1024, D])
    w_out = din("w_out", [L_, D, D])
    w_up = din("ffn_w_up", [L_, D, 2 * FFN_H])
    w_dn = din("ffn_w_down", [L_, FFN_H, D])
    k_ident = din("k_ident", [128, 128])
    k_ones = din("k_ones", [128, 128])
    k_ssdm = din("k_ssdm", [6, 128, 128])
    k_ropec = din("k_ropec", [128, 2048])
    k_ropes = din("k_ropes", [128, 2048])
    k_ropeR = din("k_ropeR", [128, 128])
    k_flipJ = din("k_flipJ", [128, 128])
    k_namask = din("k_namask", [5, 128, 640])

    y_out = dout("y_out", [T, D])
    o_nak = dout("o_nak", [L_, PT, 1024])
    o_nav = dout("o_nav", [L_, PT, 1024])
    o_gk = dout("o_gk", [L_, PT, 256])
    o_gv = dout("o_gv", [L_, PT, 256])
    o_ssd = dout("o_ssd", [2, L_, 2, 1024, 128])

    MODS = dscr("MODS", [L_, 2, 6 * D])
    XA = dscr("XA", [T, D])
    XB = dscr("XB", [T, D])
    MO = dscr("MO", [T, D])
    QAT = dscr("QAT", [8, 128, T], BF16)
    KAT = dscr("KAT", [8, 128, T], BF16)
    VAH = dscr("VAH", [8, T, 128], BF16)
    QCT = dscr("QCT", [8, 128, T], BF16)
    KCT = dscr("KCT", [2, 128, T], BF16)
    VCH = dscr("VCH", [2, T, 128], BF16)
    SZ = dscr("SZ", [T, 1024], BF16)
    XBCT = dscr("XBCT", [12, 128, T])
    DTS = dscr("DTS", [T, 32])
    GT = dscr("GT", [48, 128, T], BF16)
    OXT = [dscr("OXT%d" % i, [8, 128, T], BF16) for i in range(3)]
    MIXT = dscr("MIXT", [16, 128, T], BF16)
    RPBP = dscr("RPBP", [8 * 15 * 31 + 4096])

    st = ExitStack()
    with st:
        fw = Fw(nc, st)

        def V(fn, r=(), w=()):
            return fw.op("dve", fn, r, w)

        def A(fn, r=(), w=()):
            return fw.op("act", fn, r, w)

        def G(fn, r=(), w=()):
            return fw.op("pool", fn, r, w)

        def P(fn, r=(), w=()):
            return fw.op("pe", fn, r, w)

        def DM(out, in_, r=(), w=(), q="sp"):
            return fw.dma(lambda e: e.dma_start(out=out, in_=in_), r, w, q=q)

        evac_rr = [0]

        def EVC(dst, src, r, w):
            evac_rr[0] ^= 1
            if evac_rr[0]:
                return A(lambda e: e.activation(out=dst, in_=src, func=AF.Copy), r, w)
            return V(lambda e: e.tensor_copy(out=dst, in_=src), r, w)

        class PhaseCtx:
            def __init__(self, name):
                self.name = name

            def __enter__(self):
                self.old = fw.stack
                self.ps = ExitStack()
                self.ps.__enter__()
                fw.stack = self.ps
                return self

            def __exit__(self, *a):
                if a[0] is None:
                    fw.phase_end()
                    fw.emit()
                self.ps.__exit__(*a)
                fw.stack = self.old
                return False

        def all_dma_into(b):
            for sl in fw.dma_slots:
                if sl[1] > 0:
                    b.w[sl[0]] = max(b.w.get(sl[0], 0), sl[1])
            b.r = {}

        ident = fw.sb("ident", [128, 128], F32); b_const = fw.buf("const")
        identb = fw.sb("identb", [128, 128], BF16)
        ones = fw.sb("ones", [128, 128], F32)
        onesb = fw.sb("onesb", [128, 128], BF16)
        ssdm = fw.sb("ssdm", [128, 6, 128], F32)
        ropeR = fw.sb("ropeR", [128, 128], F32)
        flipJ = fw.sb("flipJ", [128, 128], F32)
        epsT = fw.sb("epsT", [128, 1], F32)
        sq_junk = fw.sb("sq_junk", [128, 2048], BF16); b_junk = fw.buf()
        b_c2 = fw.buf("const2")
        psF = Ring(fw, "psF", [128, 512], F32, 6, psum=True)
        psB = Ring(fw, "psB", [128, 1024], BF16, 2, psum=True)
        stat = Ring(fw, "stat", [128, 4], F32, 6)
        CONST = [b_const, b_c2]
        b_hT = fw.buf("hT")
        cur = {"hT": None}
        b_MODS = fw.buf("MODS")
        b_X = {"xin": fw.buf("xin"), "XA": fw.buf("XA"), "XB": fw.buf("XB"), "MO": fw.buf("MO")}
        b_scr = {n: fw.buf(n) for n in ("QAT", "KAT", "VAH", "QCT", "KCT", "VCH", "SZ", "XBCT", "DTS", "GT",
                                        "OXT0", "OXT1", "OXT2", "MIXT", "RPBP")}

        def scr_barrier(names):
            for n in names:
                all_dma_into(b_scr[n])

        with PhaseCtx("init"):
            DM(ident[:], k_ident, w=[b_const])
            DM(ones[:], k_ones, w=[b_const])
            DM(ssdm[:], k_ssdm.rearrange("a p n -> p a n"), w=[b_const])
            DM(ropeR[:], k_ropeR, w=[b_const])
            DM(flipJ[:], k_flipJ, w=[b_const])
            all_dma_into(b_const)
            V(lambda e: e.tensor_copy(out=identb[:], in_=ident[:]), [b_const], [b_c2])
            V(lambda e: e.tensor_copy(out=onesb[:], in_=ones[:]), [b_const], [b_c2])
            V(lambda e: e.memset(epsT[:], EPS), [], [b_c2])

        def load_T(dst_ap, n, srcs, tmpr):
            tmp, b_tmp = tmpr.next()
            for (r0, nr, ap) in srcs:
                DM(tmp[r0:r0 + nr, :], ap, w=[b_tmp])
            pt, b_pt = psF.next()
            P(lambda e: e.transpose(out=pt[:, 0:n], in_=tmp[0:n, :], identity=ident[0:n, 0:n]), [b_tmp] + CONST, [b_pt])
            b_d = fw.buf()
            V(lambda e: e.tensor_copy(out=dst_ap, in_=pt[:, 0:n]), [b_pt], [b_d])
            return b_d

        def rstd_of(src_ap, b_src, n):
            s_, b_s = stat.next()
            A(lambda e: e.activation(out=sq_junk[:, 0:n], in_=src_ap, func=AF.Square, accum_out=s_[:, 0:1]),
              [b_src], [b_junk, b_s])
            A(lambda e: e.activation(out=s_[:, 1:2], in_=s_[:, 0:1], func=AF.Sqrt, scale=1.0 / n, bias=epsT[:]),
              [b_s] + CONST, [b_s])
            V(lambda e: e.reciprocal(out=s_[:, 2:3], in_=s_[:, 1:2]), [b_s], [b_s])
            return s_[:, 2:3], b_s

        csT = fw.sb("csT", [128, 16, 2], F32)
        b_cs = fw.buf("cs")

        def phase_cs():
            with PhaseCtx("cs"):
                tmpr = Ring(fw, "tmpT", [128, 128], F32, 1)
                cs32 = fw.sb("cs32", [128, 32], F32)
                b_c = load_T(cs32[:], 32, [(0, 32, cv.rearrange("r (k p) -> (r k) p", p=128))], tmpr)
                A(lambda e: e.activation(out=cs32[:], in_=cs32[:], func=AF.Silu), [b_c], [b_c])
                V(lambda e: e.tensor_copy(out=csT[:], in_=cs32[:].rearrange("p (r k) -> p k r", r=2)), [b_c], [b_cs])

        def mods_steps(l):
            wst = Ring(fw, "mwst", [128, 16, 256], F32, 2)
            modst = Ring(fw, "modst", [2, 256], F32, 3)
            adab = Ring(fw, "adab", [2, 256], F32, 3)

            def load(ct):
                ws, b_ws = wst.next()
                DM(ws[:], ada_w[l, :, ct * 256:(ct + 1) * 256].rearrange("(k p) n -> p k n", p=128), w=[b_ws])
                ab, b_ab = adab.next()
                DM(ab[:], ada_b[l:l + 1, ct * 256:(ct + 1) * 256].partition_broadcast(2).rearrange("p a n -> p (a n)"), w=[b_ab])
                return ws, b_ws, ab, b_ab
            nxt = load(0)
            for ct in range(48):
                ws, b_ws, ab, b_ab = nxt
                nxt = load(ct + 1) if ct + 1 < 48 else None
                pt, b_pt = psF.next()
                for k in range(16):
                    P(lambda e, k=k, ws=ws, pt=pt: e.matmul(pt[0:2, 0:256], lhsT=csT[:, k, :], rhs=ws[:, k, :],
                                                             start=(k == 0), stop=(k == 15)),
                      [b_cs, b_ws], [b_pt])
                ms, b_ms = modst.next()
                V(lambda e, ms=ms, pt=pt, ab=ab: e.tensor_tensor(out=ms[:], in0=pt[0:2, 0:256], in1=ab[:], op=ALU.add),
                  [b_pt, b_ab], [b_ms])
                DM(MODS[l, :, ct * 256:(ct + 1) * 256], ms[:], r=[b_ms])
                yield
            all_dma_into(b_MODS)

        def phase_mods(l):
            with PhaseCtx("mods"):
                for _ in mods_steps(l):
                    pass

        def norm_pass(l, x_src, bx_src, add_src, b_add, gate_seg, post_nw, x_dst, bx_dst,
                      pre_l, sc_seg, sh_seg, pre_nw):
            hT = cur["hT"]
            with PhaseCtx("norm"):
                bcs = {k_: (fw.sb("bc_" + k_, [128, 2048], F32), fw.buf()) for k_ in ("g", "A", "sh", "tmp")}
                xg = Ring(fw, "xg", [128, 2048], F32, 8)
                hb = Ring(fw, "hb", [128, 2048], BF16, 2)

                def _ld_tile(tile):
                    rows_ = slice(tile * 128, (tile + 1) * 128)
                    x_t, b_x = xg.next()
                    DM(x_t[:], x_src[rows_, :], r=[bx_src], w=[b_x])
                    if add_src is not None:
                        a_t, b_a = xg.next()
                        DM(a_t[:], add_src[rows_, :], r=[b_add], w=[b_a])
                        return x_t, b_x, a_t, b_a
                    return x_t, b_x, None, None
                _pend = {}

                def load_bc(src_row_ap, rbufs=(), which="tmp"):
                    t_, b_ = bcs[which]
                    DM(t_[:], src_row_ap.partition_broadcast(128).rearrange("p a n -> p (a n)"), r=list(rbufs), w=[b_])
                    return t_, b_

                import os as _os
                _lim = int(_os.environ.get("DBG_NTILES", "99"))
                _var = _os.environ.get("DBG_VAR", "")
                for r in range(2):
                    tiles = range(0, 4) if r == 0 else range(4, NT)
                    tiles = [t_ for t_ in tiles if t_ < _lim]
                    if not tiles:
                        continue
                    if add_src is not None:
                        g_t, b_g = load_bc(MODS[l, r:r + 1, gate_seg * D:(gate_seg + 1) * D], [b_MODS], "g")
                        nw_t, b_nw = load_bc(post_nw[l:l + 1, :])
                        V(lambda e, g_t=g_t, nw_t=nw_t: e.tensor_tensor(out=g_t[:], in0=g_t[:], in1=nw_t[:], op=ALU.mult), [b_g, b_nw], [b_g])
                    if pre_l is not None:
                        A_t, b_A = load_bc(MODS[pre_l, r:r + 1, sc_seg * D:(sc_seg + 1) * D], [b_MODS], "A")
                        nw2, b_nw2 = load_bc(pre_nw[pre_l:pre_l + 1, :])
                        V(lambda e, A_t=A_t, nw2=nw2: e.scalar_tensor_tensor(out=A_t[:], in0=A_t[:], scalar=1.0, in1=nw2[:], op0=ALU.add, op1=ALU.mult),
                          [b_A, b_nw2], [b_A])
                        sh_t, b_sh = load_bc(MODS[pre_l, r:r + 1, sh_seg * D:(sh_seg + 1) * D], [b_MODS], "sh")
                    tiles = list(tiles)
                    for ti_, tile in enumerate(tiles):
                        rows = slice(tile * 128, (tile + 1) * 128)
                        if tile not in _pend:
                            _pend[tile] = _ld_tile(tile)
                        if ti_ + 1 < len(tiles) and tiles[ti_ + 1] not in _pend:
                            _pend[tiles[ti_ + 1]] = _ld_tile(tiles[ti_ + 1])
                        x_t, b_x, a_t, b_a = _pend.pop(tile)
                        if add_src is not None:
                            r_ap, b_r = rstd_of(a_t[:], b_a, 2048)
                            V(lambda e, a_t=a_t, r_ap=r_ap, g_t=g_t: e.scalar_tensor_tensor(
                                out=a_t[:], in0=a_t[:], scalar=r_ap, in1=g_t[:], op0=ALU.mult, op1=ALU.mult),
                              [b_a, b_r, b_g], [b_a])
                            G(lambda e, a_t=a_t, x_t=x_t: e.tensor_tensor(out=x_t[:], in0=x_t[:], in1=a_t[:], op=ALU.add),
                              [b_a, b_x], [b_x])
                            DM(x_dst[rows, :], x_t[:], r=[b_x])
                        if pre_l is not None:
                            r_ap, b_r = rstd_of(x_t[:], b_x, 2048)
                            tmp, b_tmp = xg.next()
                            V(lambda e, tmp=tmp, x_t=x_t, r_ap=r_ap, A_t=A_t: e.scalar_tensor_tensor(
                                out=tmp[:], in0=x_t[:], scalar=r_ap, in1=A_t[:], op0=ALU.mult, op1=ALU.mult),
                              [b_x, b_r, b_A], [b_tmp])
                            h_, b_h = hb.next()
                            G(lambda e, tmp=tmp, h_=h_, sh_t=sh_t: e.tensor_tensor(out=h_[:], in0=tmp[:], in1=sh_t[:], op=ALU.add), [b_tmp, b_sh], [b_h])
                            for half in range(2):
                                pb, b_pb = psB.next()
                                for kk in range(8):
                                    k = half * 8 + kk
                                    P(lambda e, k=k, kk=kk, pb=pb, h_=h_: e.transpose(out=pb[:, kk * 128:(kk + 1) * 128],
                                                                                      in_=h_[:, k * 128:(k + 1) * 128], identity=identb[:]),
                                      [b_h] + CONST, [b_pb])
                                EVC(hT[:, half * 8:(half + 1) * 8, tile * 128:(tile + 1) * 128],
                                    pb[:].rearrange("p (k t) -> p k t", k=8), [b_pb], [b_hT])
                if add_src is not None:
                    all_dma_into(bx_dst)

        def mm_F(pt, b_pt, wb, b_wb, kc, c0, xT, b_xT, t0, tn, kofs=0):
            for k in range(kc):
                P(lambda e, k=k: e.matmul(pt[:, 0:tn], lhsT=wb[:, k, c0:c0 + 128], rhs=xT[:, kofs + k, t0:t0 + tn],
                                          start=(k == 0), stop=(k == kc - 1)),
                  [b_wb, b_xT], [b_pt])

        def mm_T(pt, b_pt, wb, b_wb, kc, ncols, xT, b_xT, tile, kofs=0):
            for k in range(kc):
                P(lambda e, k=k: e.matmul(pt[:, 0:ncols], lhsT=xT[:, kofs + k, tile * 128:(tile + 1) * 128], rhs=wb[:, k, 0:ncols],
                                          start=(k == 0), stop=(k == kc - 1)),
                  [b_wb, b_xT], [b_pt])

        TG = [(g * 512, 512) for g in range(5)]

        class ProjCtx:
            def __init__(self, kcmax=16, nstg=4, nstgb=4):
                self.wst = Ring(fw, "wst", [128, kcmax, 256], F32, 2)
                self.wbf = Ring(fw, "wbf", [128, kcmax, 256], BF16, 2)
                self.stg = Ring(fw, "stg", [128, 512], F32, nstg)
                self.stgb = Ring(fw, "stgb", [128, 512], BF16, nstgb) if nstgb else None

            def load_w(self, src_ap_pkn, kc, ncols):
                ws, b_ws = self.wst.next()
                DM(ws[:, 0:kc, 0:ncols], src_ap_pkn, w=[b_ws])
                wb, b_wb = self.wbf.next()
                G(lambda e: e.tensor_copy(out=wb[:, 0:kc, 0:ncols], in_=ws[:, 0:kc, 0:ncols]), [b_ws], [b_wb])
                return wb, b_wb

            def stream(self, loaders):
                pc_ = self

                class _S:
                    def __init__(s_):
                        s_.i = 0
                        s_.pending = loaders[0]() if loaders else None

                    def get(s_):
                        cur = s_.pending
                        s_.i += 1
                        s_.pending = loaders[s_.i]() if s_.i < len(loaders) else None
                        return cur
                return _S()

            def store_bf(self, dst_ap, pt, b_pt, n, func=None, bias=None, bias_b=(), view=None):
                s_, b_s = self.stgb.next()
                if func is None:
                    EVC(s_[:, 0:n], pt[:, 0:n], [b_pt], [b_s])
                elif bias is None:
                    A(lambda e: e.activation(out=s_[:, 0:n], in_=pt[:, 0:n], func=func), [b_pt], [b_s])
                else:
                    A(lambda e: e.activation(out=s_[:, 0:n], in_=pt[:, 0:n], func=func, bias=bias), [b_pt] + list(bias_b), [b_s])
                src = s_[:, 0:n] if view is None else view(s_[:, 0:n])
                DM(dst_ap, src, r=[b_s])

            def store_bf2(self, dsts, pt, b_pt):
                s_, b_s = self.stgb.next()
                EVC(s_[:, 0:256], pt[:, 0:256], [b_pt], [b_s])
                for i_, d_ in enumerate(dsts):
                    DM(d_, s_[:, i_ * 128:(i_ + 1) * 128], r=[b_s])

            def store_both(self, dst32, dsts, pt, b_pt):
                if dst32 is None:
                    return self.store_bf2(dsts, pt, b_pt)
                s32, b_s32 = self.stg.next()
                EVC(s32[:, 0:256], pt[:, 0:256], [b_pt], [b_s32])
                DM(dst32, s32[:, 0:256], r=[b_s32])
                s_, b_s = self.stgb.next()
                G(lambda e: e.tensor_copy(out=s_[:, 0:256], in_=s32[:, 0:256]), [b_s32], [b_s])
                for i_, d_ in enumerate(dsts):
                    DM(d_, s_[:, i_ * 128:(i_ + 1) * 128], r=[b_s])

            def store_f32(self, dst_ap, pt, b_pt, n):
                s_, b_s = self.stg.next()
                EVC(s_[:, 0:n], pt[:, 0:n], [b_pt], [b_s])
                DM(dst_ap, s_[:, 0:n], r=[b_s])

        def phase_win(l):
            hT = cur["hT"]
            W = w_in[l]
            import os as _os
            with PhaseCtx("win"):
                pc = ProjCtx()
                stg, stgb = pc.stg, pc.stgb
                tmpr = Ring(fw, "tmpT", [128, 128], F32, 1)
                qkw = fw.sb("qkw", [128, 2], F32)
                gbT = fw.sb("gbT", [128, 48], F32)
                ropec = fw.sb("ropec", [128, 2048], F32)
                ropes = fw.sb("ropes", [128, 2048], F32)
                dtb_bc = fw.sb("dtb_bc", [128, 32], F32)
                b_rope = fw.buf(); b_dtb = fw.buf()
                DM(ropec[:], k_ropec, w=[b_rope])
                DM(ropes[:], k_ropes, w=[b_rope])
                all_dma_into(b_rope)
                b_qkw = load_T(qkw[:], 2, [(0, 1, qn_w[l:l + 1, :]), (1, 1, kn_w[l:l + 1, :])], tmpr)
                b_gbT = load_T(gbT[:], 48, [(0, 48, gate_b[l].rearrange("(c p) -> c p", p=128))], tmpr)
                DM(dtb_bc[:], dt_bias[l:l + 1, :].partition_broadcast(128).rearrange("p a n -> p (a n)"), w=[b_dtb])

                _wspecs = []
                if "qa" in _os.environ.get("DBG_WIN", "qa,va,qc,vc,z,xbc,dt,gl").split(","):
                    _wspecs += [(seg * 1024 + ct * 256, 256) for seg in (0, 1) for ct in range(4)]
                _wspecs += [(2048 + ct * 256, 256) for ct in range(4)]
                _wspecs += [(3072 + ct * 256, 256) for ct in range(4)] + [(4096, 256)]
                _wspecs += [(4352, 256)]
                _wspecs += [(4608 + ct * 256, 256) for ct in range(4)]
                _wspecs += [(5632 + ct * 256, 256) for ct in range(6)]
                _wspecs += [(7168, 32)]
                _wspecs += [(7200 + ct * 256, 256) for ct in range(24)]
                _wstream = pc.stream([(lambda c0=c0, nc_=nc_: pc.load_w(W[:, c0:c0 + nc_].rearrange("(k p) n -> p k n", p=128), 16, nc_))
                                      for (c0, nc_) in _wspecs])
                _wi = [0]

                def wtile(c0, ncols):
                    assert _wspecs[_wi[0]] == (c0, ncols), (_wspecs[_wi[0]], c0, ncols)
                    _wi[0] += 1
                    return _wstream.get()

                import os as _os
                _secs = _os.environ.get("DBG_WIN", "qa,va,qc,vc,z,xbc,dt,gl").split(",")
                for seg, dst in (((0, QAT), (1, KAT)) if "qa" in _secs else ()):
                    for ct in range(4):
                        wb, b_wb = wtile(seg * 1024 + ct * 256, 256)
                        for hh in range(2):
                            head = ct * 2 + hh
                            for (t0, tn) in TG:
                                pt, b_pt = psF.next()
                                mm_F(pt, b_pt, wb, b_wb, 16, hh * 128, hT, b_hT, t0, tn)
                                pc.store_bf(dst[head, :, t0:t0 + tn], pt, b_pt, tn)
                        if seg == 1:
                            for tile in range(4):
                                pt, b_pt = psF.next()
                                mm_T(pt, b_pt, wb, b_wb, 16, 256, hT, b_hT, tile)
                                pc.store_f32(o_nak[l, tile * 128:(tile + 1) * 128, ct * 256:(ct + 1) * 256], pt, b_pt, 256)
                for ct in (range(int(_os.environ.get("DBG_VA_CT", "4"))) if "va" in _secs else ()):
                    wb, b_wb = wtile(2048 + ct * 256, 256)
                    for tile in range(int(_os.environ.get("DBG_VA_T0", "0")), int(_os.environ.get("DBG_VA_T1", "20"))):
                        pt, b_pt = psF.next()
                        mm_T(pt, b_pt, wb, b_wb, 16, 256, hT, b_hT, tile)
                        pc.store_both(o_nav[l, tile * 128:(tile + 1) * 128, ct * 256:(ct + 1) * 256] if tile < 4 else None,
                                      [VAH[ct * 2 + hh_, tile * 128:(tile + 1) * 128, :] for hh_ in range(2)], pt, b_pt)
                for seg_c0, nheads, dst, wcol in (((3072, 8, QCT, 0), (4096, 2, KCT, 1)) if "qc" in _secs else ()):
                    for ct in range(nheads // 2):
                        wb, b_wb = wtile(seg_c0 + ct * 256, 256)
                        for hh in range(2):
                            head = ct * 2 + hh
                            for (t0, tn) in TG:
                                pt, b_pt = psF.next()
                                mm_F(pt, b_pt, wb, b_wb, 16, hh * 128, hT, b_hT, t0, tn)
                                sqb, b_sqb = stgb.next()
                                A(lambda e, sqb=sqb, pt=pt: e.activation(out=sqb[:], in_=pt[:], func=AF.Square), [b_pt], [b_sqb])
                                p2, b_p2 = psF.next()
                                P(lambda e, p2=p2, sqb=sqb: e.matmul(p2[:], lhsT=onesb[:], rhs=sqb[:], start=True, stop=True), [b_sqb] + CONST, [b_p2])
                                rs_, b_rs = stg.next()
                                A(lambda e, rs_=rs_, p2=p2: e.activation(out=rs_[:], in_=p2[:], func=AF.Ln, scale=1.0 / 128, bias=epsT[:]),
                                  [b_p2] + CONST, [b_rs])
                                A(lambda e, rs_=rs_: e.activation(out=rs_[:], in_=rs_[:], func=AF.Exp, scale=-0.5), [b_rs], [b_rs])
                                qn_, b_qn = stg.next()
                                V(lambda e, qn_=qn_, pt=pt, rs_=rs_, wcol=wcol: e.scalar_tensor_tensor(
                                    out=qn_[:], in0=pt[:], scalar=qkw[:, wcol:wcol + 1], in1=rs_[:], op0=ALU.mult, op1=ALU.mult),
                                  [b_pt, b_rs, b_qkw], [b_qn])
                                if t0 == 0:
                                    ob_, b_ob = stgb.next()
                                    G(lambda e, ob_=ob_, qn_=qn_: e.tensor_copy(out=ob_[:], in_=qn_[:]), [b_qn], [b_ob])
                                    DM(dst[head, :, t0:t0 + tn], ob_[:], r=[b_ob])
                                    if wcol == 1:
                                        p3, b_p3 = psF.next()
                                        for tt in range(4):
                                            P(lambda e, tt=tt, p3=p3, qn_=qn_: e.transpose(out=p3[:, tt * 128:(tt + 1) * 128], in_=qn_[:, tt * 128:(tt + 1) * 128],
                                                                                           identity=ident[:]), [b_qn] + CONST, [b_p3])
                                        s_, b_s = stg.next()
                                        V(lambda e, s_=s_, p3=p3: e.tensor_copy(out=s_[:], in_=p3[:]), [b_p3], [b_s])
                                        DM(o_gk[l, :, head * 128:(head + 1) * 128].rearrange("(tt p) d -> p tt d", p=128),
                                           s_[:].rearrange("p (tt d) -> p tt d", tt=4), r=[b_s])
                                else:
                                    ts0 = t0 - 512
                                    p3, b_p3 = psF.next()
                                    P(lambda e, p3=p3, qn_=qn_: e.matmul(p3[:], lhsT=ropeR[:], rhs=qn_[:], start=True, stop=True), [b_qn] + CONST, [b_p3])
                                    t2, b_t2 = stg.next()
                                    V(lambda e, t2=t2, p3=p3, ts0=ts0: e.tensor_tensor(out=t2[:], in0=p3[:], in1=ropes[:, ts0:ts0 + 512], op=ALU.mult),
                                      [b_p3, b_rope], [b_t2])
                                    G(lambda e, qn_=qn_, ts0=ts0: e.tensor_tensor(out=qn_[:], in0=qn_[:], in1=ropec[:, ts0:ts0 + 512], op=ALU.mult),
                                      [b_qn, b_rope], [b_qn])
                                    ob_, b_ob = stgb.next()
                                    V(lambda e, ob_=ob_, qn_=qn_, t2=t2: e.tensor_tensor(out=ob_[:], in0=qn_[:], in1=t2[:], op=ALU.add), [b_qn, b_t2], [b_ob])
                                    DM(dst[head, :, t0:t0 + tn], ob_[:], r=[b_ob])
                if "vc" in _secs:
                    wb, b_wb = wtile(4352, 256)
                for tile in (range(NT) if "vc" in _secs else ()):
                    pt, b_pt = psF.next()
                    mm_T(pt, b_pt, wb, b_wb, 16, 256, hT, b_hT, tile)
                    pc.store_both(o_gv[l, tile * 128:(tile + 1) * 128, :] if tile < 4 else None,
                                  [VCH[hh_, tile * 128:(tile + 1) * 128, :] for hh_ in range(2)], pt, b_pt)
                for ct in (range(4) if "z" in _secs else ()):
                    wb, b_wb = wtile(4608 + ct * 256, 256)
                    for tile in range(NT):
                        pt, b_pt = psF.next()
                        mm_T(pt, b_pt, wb, b_wb, 16, 256, hT, b_hT, tile)
                        pc.store_bf(SZ[tile * 128:(tile + 1) * 128, ct * 256:(ct + 1) * 256], pt, b_pt, 256, func=AF.Silu)
                for ct in (range(6) if "xbc" in _secs else ()):
                    wb, b_wb = wtile(5632 + ct * 256, 256)
                    for hh in range(2):
                        for (t0, tn) in TG:
                            pt, b_pt = psF.next()
                            mm_F(pt, b_pt, wb, b_wb, 16, hh * 128, hT, b_hT, t0, tn)
                            pc.store_f32(XBCT[ct * 2 + hh, :, t0:t0 + tn], pt, b_pt, tn)
                if "dt" in _secs:
                    wb, b_wb = wtile(7168, 32)
                for tile in (range(NT) if "dt" in _secs else ()):
                    pt, b_pt = psF.next()
                    mm_T(pt, b_pt, wb, b_wb, 16, 32, hT, b_hT, tile)
                    s_, b_s = stg.next()
                    V(lambda e, s_=s_, pt=pt: e.tensor_tensor(out=s_[:, 0:32], in0=pt[:, 0:32], in1=dtb_bc[:], op=ALU.add), [b_pt, b_dtb], [b_s])
                    A(lambda e, s_=s_: e.activation(out=s_[:, 32:64], in_=s_[:, 0:32], func=AF.Exp), [b_s], [b_s])
                    A(lambda e, s_=s_: e.activation(out=s_[:, 64:96], in_=s_[:, 32:64], func=AF.Ln, bias=1.0), [b_s], [b_s])
                    DM(DTS[tile * 128:(tile + 1) * 128, :], s_[:, 64:96], r=[b_s])
                for ct in (range(24) if "gl" in _secs else ()):
                    wb, b_wb = wtile(7200 + ct * 256, 256)
                    for hh in range(2):
                        cidx = ct * 2 + hh
                        for (t0, tn) in TG:
                            pt, b_pt = psF.next()
                            mm_F(pt, b_pt, wb, b_wb, 16, hh * 128, hT, b_hT, t0, tn)
                            pc.store_bf(GT[cidx, :, t0:t0 + tn], pt, b_pt, tn, func=AF.Sigmoid,
                                        bias=gbT[:, cidx:cidx + 1], bias_b=[b_gbT])
                scr_barrier(["QAT", "KAT", "VAH", "QCT", "KCT", "VCH", "SZ", "XBCT", "DTS", "GT"])

        class AttnCtx:
            def __init__(self):
                self.Sbuf = Ring(fw, "Sbuf", [128, 2560], F32, 2)
                self.Pbuf = Ring(fw, "Pbuf", [128, 2560], BF16, 2)
                self.PTb = Ring(fw, "PTb", [128, 20, 128], BF16, 2)
                self.Obuf = fw.sb("Obuf", [128, NT, 1024], BF16)
                self.b_O = [fw.buf() for _ in range(NT)]
                self.KTr = Ring(fw, "KTr", [128, T], BF16, 2)
                self.QTr = Ring(fw, "QTr", [128, T], BF16, 2)
                self.Vr = Ring(fw, "Vr", [128, NT, 128], BF16, 2)
                self.CKT = Ring(fw, "CKT", [128, 512], BF16, 2)
                self.CVb = Ring(fw, "CVb", [128, 4, 128], BF16, 2)
                self.cst = Ring(fw, "cst", [128, 4, 128], F32, 2)
                self.ost = Ring(fw, "ost", [128, 1024], BF16, 2)

            def stageA(self, qT_ap, b_q, kparts, scale):
                S_, b_S = self.Sbuf.next()
                nk = 0
                for (k_ap, kb, bias_ap, bb) in kparts:
                    n = k_ap.shape[1]
                    pt, b_pt = psF.next()
                    P(lambda e, k_ap=k_ap, pt=pt, n=n: e.matmul(pt[:, 0:n], lhsT=qT_ap, rhs=k_ap, start=True, stop=True),
                      [b_q] + list(kb), [b_pt])
                    if bias_ap is None:
                        evac_rr[0] ^= 1
                        if evac_rr[0]:
                            A(lambda e, pt=pt, n=n, nk=nk: e.activation(out=S_[:, nk:nk + n], in_=pt[:, 0:n], func=AF.Copy, scale=scale), [b_pt], [b_S])
                        else:
                            V(lambda e, pt=pt, n=n, nk=nk: e.tensor_scalar(out=S_[:, nk:nk + n], in0=pt[:, 0:n], scalar1=scale, scalar2=None, op0=ALU.mult),
                              [b_pt], [b_S])
                    else:
                        V(lambda e, pt=pt, n=n, nk=nk, bias_ap=bias_ap: e.scalar_tensor_tensor(
                            out=S_[:, nk:nk + n], in0=pt[:, 0:n], scalar=scale, in1=bias_ap, op0=ALU.mult, op1=ALU.add),
                          [b_pt] + list(bb), [b_S])
                    nk += n
                s_, b_s = stat.next()
                V(lambda e: e.tensor_reduce(out=s_[:, 0:1], in_=S_[:, 0:nk], axis=AX.X, op=ALU.max, negate=True), [b_S], [b_s])
                P_, b_P = self.Pbuf.next()
                A(lambda e: e.activation(out=P_[:, 0:nk], in_=S_[:, 0:nk], func=AF.Exp, bias=s_[:, 0:1], accum_out=s_[:, 1:2]),
                  [b_S, b_s], [b_P, b_s])
                V(lambda e: e.reciprocal(out=s_[:, 2:3], in_=s_[:, 1:2]), [b_s], [b_s])
                return (P_, b_P, s_, b_s, nk)

            def stageB(self, st_, vblocks, o_ap, b_o):
                P_, b_P, s_, b_s, nk = st_
                nb = nk // 128
                PT_, b_PT = self.PTb.next()
                for b0 in range(0, nb, 8):
                    bn = min(8, nb - b0)
                    pb, b_pb = psB.next()
                    for i in range(bn):
                        P(lambda e, i=i, pb=pb, b0=b0: e.transpose(out=pb[:, i * 128:(i + 1) * 128],
                                                                   in_=P_[:, (b0 + i) * 128:(b0 + i + 1) * 128], identity=identb[:]),
                          [b_P] + CONST, [b_pb])
                    EVC(PT_[:, b0:b0 + bn, :], pb[:, 0:bn * 128].rearrange("p (k t) -> p k t", k=bn), [b_pb], [b_PT])
                po, b_po = psF.next()
                for i, (v_ap, vb) in enumerate(vblocks):
                    P(lambda e, i=i, v_ap=v_ap: e.matmul(po[:, 0:128], lhsT=PT_[:, i, :], rhs=v_ap, start=(i == 0), stop=(i == nb - 1)),
                      [b_PT] + list(vb), [b_po])
                V(lambda e: e.tensor_scalar(out=o_ap, in0=po[:, 0:128], scalar1=s_[:, 2:3], scalar2=None, op0=ALU.mult),
                  [b_po, b_s], [b_o])

            def run(self, jobs, scale, hook=None):
                prev = None
                for job in jobs:
                    qT_ap, b_q, kparts, vblocks, o_ap, b_o = job
                    st_ = self.stageA(qT_ap, b_q, kparts, scale)
                    if prev is not None:
                        self.stageB(*prev)
                        if hook is not None:
                            hook()
                    prev = (st_, vblocks, o_ap, b_o)
                if prev is not None:
                    self.stageB(*prev)
                    if hook is not None:
                        hook()

            def tile(self, qT_ap, b_q, kparts, vblocks, scale, o_ap, b_o):
                self.run([(qT_ap, b_q, kparts, vblocks, o_ap, b_o)], scale)

            def o_store(self, dst, nm):
                for tile in range(NT):
                    pb, b_pb = psB.next()
                    for hh in range(8):
                        P(lambda e, hh=hh, pb=pb, tile=tile: e.transpose(out=pb[:, hh * 128:(hh + 1) * 128], in_=self.Obuf[:, tile, hh * 128:(hh + 1) * 128],
                                                                         identity=identb[:]), [self.b_O[tile]] + CONST, [b_pb])
                    s_, b_s = self.ost.next()
                    EVC(s_[:], pb[:], [b_pb], [b_s])
                    DM(dst[:, :, tile * 128:(tile + 1) * 128].rearrange("c p t -> p c t"),
                       s_[:].rearrange("p (c t) -> p c t", c=8), r=[b_s])
                scr_barrier([nm])

            def load_ctx(self, cache_k_ap, cache_v_ap):
                c1, b_c1 = self.cst.next()
                DM(c1[:], cache_k_ap.rearrange("(tt p) d -> p tt d", p=128), w=[b_c1])
                p3, b_p3 = psF.next()
                for tt in range(4):
                    P(lambda e, tt=tt: e.transpose(out=p3[:, tt * 128:(tt + 1) * 128], in_=c1[:, tt, :], identity=ident[:]),
                      [b_c1] + CONST, [b_p3])
                ck, b_ck = self.CKT.next()
                V(lambda e: e.tensor_copy(out=ck[:], in_=p3[:]), [b_p3], [b_ck])
                c2, b_c2_ = self.cst.next()
                DM(c2[:], cache_v_ap.rearrange("(tt p) d -> p tt d", p=128), w=[b_c2_])
                cvb, b_cv = self.CVb.next()
                G(lambda e: e.tensor_copy(out=cvb[:], in_=c2[:]), [b_c2_], [b_cv])
                return ck, b_ck, cvb, b_cv

        def phase_na(l):
            scale = 128 ** -0.5
            with PhaseCtx("na"):
                ac = AttnCtx()
                biasm = fw.sb("biasm", [128, 5, 640], F32); b_biasm = fw.buf()
                namask = fw.sb("namask", [128, 5, 640], F32); b_namask = fw.buf()
                G2p = fw.sb("G2p", [128, 18 * 64], F32); b_G2p = fw.buf()
                G2 = fw.sb("G2", [128, 18 * 64], F32); b_G2 = fw.buf()
                zeroT = fw.sb("zeroT", [1, 2048], F32); b_z = fw.buf()
                V(lambda e: e.memset(zeroT[:], 0.0), [], [b_z])
                DM(namask[:], k_namask.rearrange("a p n -> p a n"), w=[b_namask])
                bR = b_scr["RPBP"]
                DM(RPBP[0:2048].rearrange("(p n) -> p n", p=1), zeroT[:], r=[b_z, bR], w=[bR])
                DM(RPBP[2048 + 3720:2048 + 3720 + 2048].rearrange("(p n) -> p n", p=1), zeroT[:], r=[b_z])
                DM(RPBP[2048:2048 + 3720].rearrange("(p n) -> p n", p=1), na_rpb[l:l + 1, :])
                scr_barrier(["RPBP"])
                for head in range(8):
                    kt, b_kt = ac.KTr.next()
                    DM(kt[:], KAT[head], r=[b_scr["KAT"]], w=[b_kt])
                    qt, b_qt = ac.QTr.next()
                    DM(qt[:], QAT[head], r=[b_scr["QAT"]], w=[b_qt])
                    vt, b_vt = ac.Vr.next()
                    DM(vt[:], VAH[head].rearrange("(n p) d -> p n d", p=128), r=[b_scr["VAH"]], w=[b_vt])
                    ck, b_ck, cvb, b_cv = ac.load_ctx(c_nak[l, :, head, :], c_nav[l, :, head, :])
                    evs = []
                    for half in range(2):
                        base = 2048 + head * 465 + (-1 - half) * 31 - 48
                        src = AP(RPBP.tensor, base, [[1, 64], [31, 18], [1, 64]])
                        DM(G2p[half * 64:(half + 1) * 64, :].rearrange("p (a c) -> p a c", a=18), src, r=[bR], w=[b_G2p] if half == 0 else [])
                    all_dma_into(b_G2p)
                    for c0 in range(0, 1152, 512):
                        n = min(512, 1152 - c0)
                        pt, b_pt = psF.next()
                        P(lambda e, pt=pt, c0=c0, n=n: e.matmul(pt[:, 0:n], lhsT=flipJ[:], rhs=G2p[:, c0:c0 + n], start=True, stop=True),
                          [b_G2p] + CONST, [b_pt])
                        V(lambda e, pt=pt, c0=c0, n=n: e.tensor_copy(out=G2[:, c0:c0 + n], in_=pt[:, 0:n]), [b_pt], [b_G2])
                    for ti, jt in enumerate([0, 1, 5, 14, 15]):
                        tw = min(max(jt - 2, 0), 11)
                        a0 = 2 * (tw - jt) + 7 + 1
                        V(lambda e, ti=ti, a0=a0: e.tensor_tensor(out=biasm[:, ti, :], in0=G2[:, a0 * 64:(a0 + 10) * 64],
                                                                  in1=namask[:, ti, :], op=ALU.add),
                          [b_G2, b_namask], [b_biasm])
                    jobs = []
                    for (s0, sl, r) in SEQS[:2]:
                        for tile in range(s0 // 128, (s0 + sl) // 128):
                            jobs.append((qt[:, tile * 128:(tile + 1) * 128], b_qt,
                                         [(kt[:, s0:s0 + sl], [b_kt], None, [])],
                                         [(vt[:, s0 // 128 + i, :], [b_vt]) for i in range(sl // 128)],
                                         ac.Obuf[:, tile, head * 128:(head + 1) * 128], ac.b_O[tile]))
                    for jt in range(16):
                        tile = 4 + jt
                        tw = min(max(jt - 2, 0), 11)
                        ti = na_tile_type(jt)
                        k0 = 512 + tw * 128
                        kparts = [(kt[:, k0:k0 + 512], [b_kt], biasm[:, ti, 0:512], [b_biasm]),
                                  (kt[:, k0 + 512:k0 + 640], [b_kt], biasm[:, ti, 512:640], [b_biasm]),
                                  (ck[:], [b_ck], None, [])]
                        vbl = [(vt[:, 4 + tw + i, :], [b_vt]) for i in range(5)] + [(cvb[:, i, :], [b_cv]) for i in range(4)]
                        jobs.append((qt[:, tile * 128:(tile + 1) * 128], b_qt, kparts, vbl,
                                     ac.Obuf[:, tile, head * 128:(head + 1) * 128], ac.b_O[tile]))
                    ac.run(jobs, scale)
                ac.o_store(OXT[0], "OXT0")

        def phase_gqa(l):
            scale = 128 ** -0.5
            with PhaseCtx("gqa"):
                ac = AttnCtx()
                mgen = mods_steps(l + 1) if l + 1 < depth else iter(())
                mcnt = [0]

                def mstep():
                    mcnt[0] += 1
                    if mcnt[0] % 3 == 0:
                        next(mgen, None)
                for kv in range(2):
                    kt, b_kt = ac.KTr.next()
                    DM(kt[:], KCT[kv], r=[b_scr["KCT"]], w=[b_kt])
                    vt, b_vt = ac.Vr.next()
                    DM(vt[:], VCH[kv].rearrange("(n p) d -> p n d", p=128), r=[b_scr["VCH"]], w=[b_vt])
                    ck, b_ck, cvb, b_cv = ac.load_ctx(c_gk[l, :, kv, :], c_gv[l, :, kv, :])
                    for g in range(4):
                        head = kv * 4 + g
                        qt, b_qt = ac.QTr.next()
                        DM(qt[:], QCT[head], r=[b_scr["QCT"]], w=[b_qt])
                        jobs = []
                        for (s0, sl, r) in SEQS[:2]:
                            for tile in range(s0 // 128, (s0 + sl) // 128):
                                jobs.append((qt[:, tile * 128:(tile + 1) * 128], b_qt,
                                             [(kt[:, s0:s0 + sl], [b_kt], None, [])],
                                             [(vt[:, s0 // 128 + i, :], [b_vt]) for i in range(sl // 128)],
                                             ac.Obuf[:, tile, head * 128:(head + 1) * 128], ac.b_O[tile]))
                        for jt in range(16):
                            tile = 4 + jt
                            kparts = [(kt[:, 512 + i * 512:512 + (i + 1) * 512], [b_kt], None, []) for i in range(4)]
                            kparts.append((ck[:], [b_ck], None, []))
                            vbl = [(vt[:, 4 + i, :], [b_vt]) for i in range(16)] + [(cvb[:, i, :], [b_cv]) for i in range(4)]
                            jobs.append((qt[:, tile * 128:(tile + 1) * 128], b_qt, kparts, vbl,
                                         ac.Obuf[:, tile, head * 128:(head + 1) * 128], ac.b_O[tile]))
                        ac.run(jobs, scale, hook=mstep)
                for _ in mgen:
                    pass
                ac.o_store(OXT[2], "OXT2")

        def phase_ssd(l):
            with PhaseCtx("ssd"):
                xtok = fw.sb("xtok", [128, 16, 1024], BF16); b_xtok = [fw.buf() for _ in range(16)]
                yacc = fw.sb("yacc", [128, 16, 1024], F32); b_yacc = [fw.buf() for _ in range(16)]
                Btok = fw.sb("Btok", [128, 16, 256], BF16); b_Btok = [fw.buf() for _ in range(16)]
                BTs = fw.sb("BTs", [128, 2, 2048], BF16); b_BTs = [fw.buf() for _ in range(4)]
                CTs = fw.sb("CTs", [128, 2, 2048], BF16); b_CTs = [fw.buf() for _ in range(4)]
                dts = fw.sb("dts", [128, 16, 32], F32); b_dts = fw.buf()
                cwT = fw.sb("cwT", [128, 48], F32)
                Aneg = fw.sb("Aneg", [128, 32], F32); b_Aneg = fw.buf()
                dtot = fw.sb("dtot", [128, 16], F32); b_dtot = fw.buf()
                dsk = fw.sb("dsk", [128, 32], F32)
                hst = fw.sb("hst", [128, 1024], F32); b_hst = fw.buf()
                hstb = fw.sb("hstb", [128, 1024], BF16); b_hstb = fw.buf()
                ssdnw = fw.sb("ssdnw", [128, 1024], F32); b_ssdnw = fw.buf()
                st_in = fw.sb("st_in", [128, 8, 128], F32); b_stin = fw.buf()
                xcin = Ring(fw, "xcin", [128, 514], F32, 2)
                xc1 = Ring(fw, "xc1", [128, 512], F32, 2)
                xcb = Ring(fw, "xcb", [128, 512], BF16, 2)
                small = Ring(fw, "small", [128, 64], F32, 4)
                lhs_r = Ring(fw, "lhs_r", [128, 128], F32, 4)
                Lr = Ring(fw, "Lr", [128, 512], F32, 5)
                cbm = Ring(fw, "cbm", [128, 2, 128], F32, 2)
                MTr = Ring(fw, "MTr", [128, 128], BF16, 4)
                xwr = Ring(fw, "xwr", [128, 1024], BF16, 2)
                tmpy = Ring(fw, "tmpy", [128, 512], F32, 2)
                szr = Ring(fw, "szr", [128, 1024], BF16, 1)
                obr = Ring(fw, "obr", [128, 1024], BF16, 1)
                t1r = Ring(fw, "t1r", [128, 1024], F32, 1)
                ost = Ring(fw, "ost", [128, 1024], BF16, 2)
                stg = Ring(fw, "stg", [128, 512], F32, 2)
                tmpr = Ring(fw, "tmpT", [128, 128], F32, 1)
                b_cw = load_T(cwT[:], 48, [(0, 36, conv_w[l].rearrange("j (c p) -> (j c) p", p=128)),
                                           (36, 12, conv_b[l].rearrange("(c p) -> c p", p=128))], tmpr)
                DM(Aneg[:], a_log[l:l + 1, :].partition_broadcast(128).rearrange("p a n -> p (a n)"), w=[b_Aneg])
                A(lambda e: e.activation(out=Aneg[:], in_=Aneg[:], func=AF.Exp), [b_Aneg], [b_Aneg])
                V(lambda e: e.tensor_scalar(out=Aneg[:], in0=Aneg[:], scalar1=-1.0, scalar2=None, op0=ALU.mult), [b_Aneg], [b_Aneg])
                DM(dsk[:], ssd_d[l:l + 1, :].partition_broadcast(128).rearrange("p a n -> p (a n)"), w=[b_dtot])
                V(lambda e: e.tensor_tensor(out=dtot[:], in0=dsk[:, 0:16], in1=dsk[:, 16:32], op=ALU.add), [b_dtot], [b_dtot])
                DM(ssdnw[:], ssd_nw[l:l + 1, :].partition_broadcast(128).rearrange("p a n -> p (a n)"), w=[b_ssdnw])
                for si, (s0, sl, r) in enumerate(SEQS):
                    nch = sl // 128
                    for b0 in range(0, sl, 512):
                        bl = min(512, sl - b0)
                        gi = b0 // 512
                        ch0 = b0 // 128
                        nt_ = bl // 128
                        for c in range(12):
                            xi, b_xi = xcin.next()
                            lo = 1 if b0 == 0 else 0
                            hi = 1 if b0 + bl == sl else 0
                            if lo:
                                V(lambda e, xi=xi: e.memset(xi[:, 0:1], 0.0), [], [b_xi])
                            if hi:
                                V(lambda e, xi=xi, bl=bl: e.memset(xi[:, bl + 1:bl + 2], 0.0), [], [b_xi])
                            DM(xi[:, lo:bl + 2 - hi], XBCT[c, :, s0 + b0 - 1 + lo:s0 + b0 + bl + 1 - hi], r=[b_scr["XBCT"]], w=[b_xi])
                            x1, b_x1 = xc1.next()
                            V(lambda e, xi=xi, x1=x1, c=c, bl=bl: e.tensor_scalar(out=x1[:, 0:bl], in0=xi[:, 0:bl], scalar1=cwT[:, c:c + 1],
                                                                                  scalar2=cwT[:, 36 + c:37 + c], op0=ALU.mult, op1=ALU.add),
                              [b_xi, b_cw], [b_x1])
                            V(lambda e, xi=xi, x1=x1, c=c, bl=bl: e.scalar_tensor_tensor(out=x1[:, 0:bl], in0=xi[:, 1:bl + 1], scalar=cwT[:, 12 + c:13 + c],
                                                                                         in1=x1[:, 0:bl], op0=ALU.mult, op1=ALU.add),
                              [b_xi, b_cw, b_x1], [b_x1])
                            V(lambda e, xi=xi, x1=x1, c=c, bl=bl: e.scalar_tensor_tensor(out=x1[:, 0:bl], in0=xi[:, 2:bl + 2], scalar=cwT[:, 24 + c:25 + c],
                                                                                         in1=x1[:, 0:bl], op0=ALU.mult, op1=ALU.add),
                              [b_xi, b_cw, b_x1], [b_x1])
                            if c < 8:
                                xb_, b_xb = xcb.next()
                                A(lambda e, x1=x1, xb_=xb_, bl=bl: e.activation(out=xb_[:, 0:bl], in_=x1[:, 0:bl], func=AF.Silu), [b_x1], [b_xb])
                                pb, b_pb = psB.next()
                                for tt in range(nt_):
                                    P(lambda e, tt=tt, pb=pb, xb_=xb_: e.transpose(out=pb[:, tt * 128:(tt + 1) * 128],
                                                                                   in_=xb_[:, tt * 128:(tt + 1) * 128], identity=identb[:]),
                                      [b_xb] + CONST, [b_pb])
                                V(lambda e, pb=pb, ch0=ch0, nt_=nt_, c=c: e.tensor_copy(
                                    out=xtok[:, ch0:ch0 + nt_, c * 128:(c + 1) * 128],
                                    in_=pb[:, 0:nt_ * 128].rearrange("p (t d) -> p t d", t=nt_)), [b_pb], [b_xtok[ch0 + tt] for tt in range(nt_)])
                            elif c < 10:
                                g_ = c - 8
                                A(lambda e, x1=x1, g_=g_, b0=b0, bl=bl: e.activation(out=BTs[:, g_, b0:b0 + bl], in_=x1[:, 0:bl], func=AF.Silu),
                                  [b_x1], [b_BTs[gi]])
                                pb, b_pb = psB.next()
                                for tt in range(nt_):
                                    P(lambda e, tt=tt, pb=pb, g_=g_, b0=b0: e.transpose(out=pb[:, tt * 128:(tt + 1) * 128],
                                                                                        in_=BTs[:, g_, b0 + tt * 128:b0 + (tt + 1) * 128], identity=identb[:]),
                                      [b_BTs[gi]] + CONST, [b_pb])
                                V(lambda e, pb=pb, ch0=ch0, nt_=nt_, g_=g_: e.tensor_copy(
                                    out=Btok[:, ch0:ch0 + nt_, g_ * 128:(g_ + 1) * 128],
                                    in_=pb[:, 0:nt_ * 128].rearrange("p (t d) -> p t d", t=nt_)), [b_pb], [b_Btok[ch0 + tt] for tt in range(nt_)])
                            else:
                                g_ = c - 10
                                A(lambda e, x1=x1, g_=g_, b0=b0, bl=bl: e.activation(out=CTs[:, g_, b0:b0 + bl], in_=x1[:, 0:bl], func=AF.Silu),
                                  [b_x1], [b_CTs[gi]])
                    DM(dts[:, 0:nch, :], DTS[s0:s0 + sl, :].rearrange("(n p) d -> p n d", p=128), r=[b_scr["DTS"]], w=[b_dts])
                    def run_dir(di, si=si, s0=s0, sl=sl, r=r, nch=nch):
                        Um = ssdm[:, di * 3 + 0, :]
                        TRI = ssdm[:, di * 3 + 1, :]
                        MSK = ssdm[:, di * 3 + 2, :]
                        ce = 127 if di == 0 else 0
                        if r == 0:
                            V(lambda e: e.memset(hst[:], 0.0), [], [b_hst])
                        else:
                            DM(st_in[:], c_ssd[l, di].rearrange("(c q) n -> q c n", q=128), w=[b_stin])
                            for half in range(2):
                                pt, b_pt = psF.next()
                                for cc in range(4):
                                    c8 = half * 4 + cc
                                    P(lambda e, cc=cc, c8=c8, pt=pt: e.transpose(out=pt[:, cc * 128:(cc + 1) * 128], in_=st_in[:, c8, :], identity=ident[:]),
                                      [b_stin] + CONST, [b_pt])
                                V(lambda e, pt=pt, half=half: e.tensor_copy(out=hst[:, half * 512:(half + 1) * 512], in_=pt[:]), [b_pt], [b_hst])
                        chunks = range(nch) if di == 0 else range(nch - 1, -1, -1)
                        for c in chunks:
                            gi = c // 4
                            sm, b_sm = small.next()
                            V(lambda e, sm=sm, c=c: e.tensor_tensor(out=sm[:, 0:16], in0=dts[:, c, di * 16:(di + 1) * 16], in1=Aneg[:, di * 16:(di + 1) * 16],
                                                                    op=ALU.mult), [b_dts, b_Aneg], [b_sm])
                            pt, b_pt = psF.next()
                            P(lambda e, pt=pt, sm=sm: e.matmul(pt[:, 0:16], lhsT=TRI, rhs=sm[:, 0:16], start=True, stop=True), [b_sm] + CONST, [b_pt])
                            P(lambda e, pt=pt, sm=sm: e.matmul(pt[:, 16:32], lhsT=ones[:], rhs=sm[:, 0:16], start=True, stop=True), [b_sm] + CONST, [b_pt])
                            A(lambda e, pt=pt, sm=sm: e.activation(out=sm[:, 16:48], in_=pt[:, 0:32], func=AF.Exp), [b_pt], [b_sm])
                            cb_, b_cb = cbm.next()
                            pt2, b_pt2 = psF.next()
                            for g_ in range(2):
                                P(lambda e, g_=g_, pt2=pt2, c=c: e.matmul(pt2[:, g_ * 128:(g_ + 1) * 128], lhsT=BTs[:, g_, c * 128:(c + 1) * 128],
                                                                          rhs=CTs[:, g_, c * 128:(c + 1) * 128], start=True, stop=True),
                                  [b_BTs[gi], b_CTs[gi]], [b_pt2])
                            V(lambda e, cb_=cb_, pt2=pt2: e.tensor_tensor(out=cb_[:], in0=pt2[:, 0:256].rearrange("p (g l) -> p g l", g=2),
                                                                          in1=MSK.unsqueeze(1).to_broadcast([128, 2, 128]), op=ALU.mult),
                              [b_pt2] + CONST, [b_cb])
                            G(lambda e: e.tensor_copy(out=hstb[:], in_=hst[:]), [b_hst], [b_hstb])
                            pyo = []
                            for g_ in range(2):
                                po, b_po = psF.next()
                                P(lambda e, g_=g_, po=po, c=c: e.matmul(po[:], lhsT=CTs[:, g_, c * 128:(c + 1) * 128], rhs=hstb[:, g_ * 512:(g_ + 1) * 512],
                                                                        start=True, stop=True), [b_CTs[gi], b_hstb], [b_po])
                                ty, b_ty = tmpy.next()
                                V(lambda e, g_=g_, po=po, ty=ty, sm=sm: e.tensor_tensor(
                                    out=ty[:].rearrange("p (h q) -> p h q", h=8), in0=po[:].rearrange("p (h q) -> p h q", h=8),
                                    in1=sm[:, 16 + g_ * 8:16 + (g_ + 1) * 8].unsqueeze(2).to_broadcast([128, 8, 64]), op=ALU.mult),
                                  [b_po, b_sm], [b_ty])
                                pyo.append((ty, b_ty))
                            Ls = []
                            for q4 in range(4):
                                pl, b_pl = psF.next()
                                for hh in range(4):
                                    h_ = q4 * 4 + hh
                                    lh, b_lh = lhs_r.next()
                                    G(lambda e, lh=lh, h_=h_, sm=sm: e.tensor_scalar(out=lh[:], in0=Um, scalar1=sm[:, h_:h_ + 1], scalar2=None, op0=ALU.mult),
                                      [b_sm] + CONST, [b_lh])
                                    P(lambda e, lh=lh, pl=pl, hh=hh: e.matmul(pl[:, hh * 128:(hh + 1) * 128], lhsT=lh[:], rhs=TRI, start=True, stop=True),
                                      [b_lh] + CONST, [b_pl])
                                Lq, b_L = Lr.next()
                                A(lambda e, Lq=Lq, pl=pl: e.activation(out=Lq[:], in_=pl[:], func=AF.Exp), [b_pl], [b_L])
                                Ls.append((Lq, b_L))
                            sw, b_sw = small.next()
                            for q4 in range(4):
                                Lq, b_L = Ls[q4]
                                V(lambda e, Lq=Lq, sw=sw, q4=q4, c=c: e.tensor_tensor(
                                    out=sw[:, q4 * 4:(q4 + 1) * 4], in0=Lq[:].rearrange("p (h l) -> p h l", h=4)[:, :, ce],
                                    in1=dts[:, c, di * 16 + q4 * 4:di * 16 + (q4 + 1) * 4], op=ALU.mult),
                                  [b_L, b_dts], [b_sw])
                            for g_ in range(2):
                                pd, b_pd = psF.next()
                                for hh in range(8):
                                    h_ = g_ * 8 + hh
                                    Lq, b_L = Ls[h_ // 4]
                                    mt, b_mt = MTr.next()
                                    V(lambda e, mt=mt, Lq=Lq, h_=h_, g_=g_, cb_=cb_, c=c: e.scalar_tensor_tensor(
                                        out=mt[:], in0=Lq[:, (h_ % 4) * 128:(h_ % 4 + 1) * 128], scalar=dts[:, c, di * 16 + h_:di * 16 + h_ + 1],
                                        in1=cb_[:, g_, :], op0=ALU.mult, op1=ALU.mult), [b_L, b_dts, b_cb], [b_mt])
                                    P(lambda e, mt=mt, pd=pd, hh=hh, h_=h_, c=c: e.matmul(pd[:, hh * 64:(hh + 1) * 64], lhsT=mt[:],
                                                                                         rhs=xtok[:, c, h_ * 64:(h_ + 1) * 64], start=True, stop=True),
                                      [b_mt, b_xtok[c]], [b_pd])
                                ty, b_ty = pyo[g_]
                                if di == 0:
                                    V(lambda e, g_=g_, pd=pd, ty=ty, c=c: e.tensor_tensor(out=yacc[:, c, g_ * 512:(g_ + 1) * 512], in0=pd[:], in1=ty[:], op=ALU.add),
                                      [b_pd, b_ty], [b_yacc[c]])
                                else:
                                    V(lambda e, pd=pd, ty=ty: e.tensor_tensor(out=ty[:], in0=pd[:], in1=ty[:], op=ALU.add), [b_pd, b_ty], [b_ty])
                                    G(lambda e, g_=g_, ty=ty, c=c: e.tensor_tensor(out=yacc[:, c, g_ * 512:(g_ + 1) * 512], in0=yacc[:, c, g_ * 512:(g_ + 1) * 512],
                                                                                  in1=ty[:], op=ALU.add), [b_ty, b_yacc[c]], [b_yacc[c]])
                            xw, b_xw = xwr.next()
                            G(lambda e, xw=xw, sw=sw, c=c: e.tensor_tensor(out=xw[:].rearrange("p (h q) -> p h q", h=16),
                                                                           in0=xtok[:, c, :].rearrange("p (h q) -> p h q", h=16),
                                                                           in1=sw[:, 0:16].unsqueeze(2).to_broadcast([128, 16, 64]), op=ALU.mult),
                              [b_xtok[c], b_sw], [b_xw])
                            G(lambda e, sm=sm: e.tensor_tensor(out=hst[:].rearrange("p (h q) -> p h q", h=16), in0=hst[:].rearrange("p (h q) -> p h q", h=16),
                                                               in1=sm[:, 32:48].unsqueeze(2).to_broadcast([128, 16, 64]), op=ALU.mult),
                              [b_hst, b_sm], [b_hst])
                            for g_ in range(2):
                                pst, b_pst = psF.next()
                                P(lambda e, g_=g_, pst=pst, xw=xw, c=c: e.matmul(pst[:], lhsT=Btok[:, c, g_ * 128:(g_ + 1) * 128], rhs=xw[:, g_ * 512:(g_ + 1) * 512],
                                                                                start=True, stop=True), [b_Btok[c], b_xw], [b_pst])
                                V(lambda e, g_=g_, pst=pst: e.tensor_tensor(out=hst[:, g_ * 512:(g_ + 1) * 512], in0=hst[:, g_ * 512:(g_ + 1) * 512], in1=pst[:], op=ALU.add),
                                  [b_pst, b_hst], [b_hst])
                        if r == 0:
                            for half in range(2):
                                pt, b_pt = psF.next()
                                for cc in range(4):
                                    c8 = half * 4 + cc
                                    P(lambda e, cc=cc, c8=c8, pt=pt: e.transpose(out=pt[:, cc * 128:(cc + 1) * 128], in_=hst[:, c8 * 128:(c8 + 1) * 128], identity=ident[:]),
                                      [b_hst] + CONST, [b_pt])
                                s_, b_s = stg.next()
                                V(lambda e, pt=pt, s_=s_: e.tensor_copy(out=s_[:], in_=pt[:]), [b_pt], [b_s])
                                DM(o_ssd[si, l, di, half * 512:(half + 1) * 512, :].rearrange("(c q) n -> q c n", q=128),
                                   s_[:].rearrange("p (c n) -> p c n", c=4), r=[b_s])
                    run_dir(0)
                    run_dir(1)
                    for c in range(nch):
                        tile = s0 // 128 + c
                        sz_, b_sz = szr.next()
                        DM(sz_[:], SZ[tile * 128:(tile + 1) * 128, :], r=[b_scr["SZ"]], w=[b_sz])
                        t1, b_t1 = t1r.next()
                        V(lambda e, t1=t1, c=c: e.tensor_tensor(out=t1[:].rearrange("p (h q) -> p h q", h=16),
                                                                in0=xtok[:, c, :].rearrange("p (h q) -> p h q", h=16),
                                                                in1=dtot[:].unsqueeze(2).to_broadcast([128, 16, 64]), op=ALU.mult),
                          [b_xtok[c], b_dtot], [b_t1])
                        G(lambda e, t1=t1, c=c: e.tensor_tensor(out=t1[:], in0=t1[:], in1=yacc[:, c, :], op=ALU.add), [b_t1, b_yacc[c]], [b_t1])
                        V(lambda e, t1=t1, sz_=sz_: e.tensor_tensor(out=t1[:], in0=t1[:], in1=sz_[:], op=ALU.mult), [b_t1, b_sz], [b_t1])
                        r_ap, b_r = rstd_of(t1[:], b_t1, 1024)
                        ob_, b_ob = obr.next()
                        V(lambda e, t1=t1, r_ap=r_ap, ob_=ob_: e.scalar_tensor_tensor(out=ob_[:], in0=t1[:], scalar=r_ap, in1=ssdnw[:],
                                                                                      op0=ALU.mult, op1=ALU.mult), [b_t1, b_r, b_ssdnw], [b_ob])
                        pb, b_pb = psB.next()
                        for hh in range(8):
                            P(lambda e, hh=hh, pb=pb, ob_=ob_: e.transpose(out=pb[:, hh * 128:(hh + 1) * 128], in_=ob_[:, hh * 128:(hh + 1) * 128], identity=identb[:]),
                              [b_ob] + CONST, [b_pb])
                        s_, b_s = ost.next()
                        EVC(s_[:], pb[:], [b_pb], [b_s])
                        DM(OXT[1][:, :, tile * 128:(tile + 1) * 128].rearrange("c p t -> p c t"),
                           s_[:].rearrange("p (c t) -> p c t", c=8), r=[b_s])
                scr_barrier(["OXT1"])

        def phase_branch(l):
            hT = cur["hT"]
            with PhaseCtx("branch"):
                pc = ProjCtx(kcmax=8)
                stg, stgb = pc.stg, pc.stgb
                ox2 = fw.sb("ox2", [128, 8, T], BF16); b_ox = fw.buf()
                DM(hT[:, 0:8, :], OXT[0].rearrange("c p t -> p c t"), r=[b_scr["OXT0"]], w=[b_hT])
                DM(hT[:, 8:16, :], OXT[1].rearrange("c p t -> p c t"), r=[b_scr["OXT1"]])
                all_dma_into(b_hT)
                DM(ox2[:], OXT[2].rearrange("c p t -> p c t"), r=[b_scr["OXT2"]], w=[b_ox])
                srcs = [(hT, b_hT, 0), (hT, b_hT, 8), (ox2, b_ox, 0)]
                wbr = [Ring(fw, "wbr%d" % i, [128, 8, 256], BF16, 2) for i in range(3)]
                def _ld3(ct):
                    wts_ = []
                    for br in range(3):
                        ws, b_ws = pc.wst.next()
                        DM(ws[:, 0:8, :], w_br[l, br, :, ct * 256:(ct + 1) * 256].rearrange("(k p) n -> p k n", p=128), w=[b_ws])
                        wb, b_wb = wbr[br].next()
                        G(lambda e, wb=wb, ws=ws: e.tensor_copy(out=wb[:], in_=ws[:, 0:8, :]), [b_ws], [b_wb])
                        wts_.append((wb, b_wb))
                    return wts_
                _ws = pc.stream([(lambda ct=ct: _ld3(ct)) for ct in range(8)])
                gtr = Ring(fw, "gtr", [128, 512], BF16, 9)

                def _ldg(j, t0, tn):
                    g3 = []
                    for br in range(3):
                        gt, b_gt = gtr.next()
                        DM(gt[:], GT[br * 16 + j, :, t0:t0 + tn], r=[b_scr["GT"]], w=[b_gt])
                        g3.append((gt, b_gt))
                    return g3
                _gs = pc.stream([(lambda j=ct * 2 + hh, t0=t0, tn=tn: _ldg(j, t0, tn)) for ct in range(8) for hh in range(2) for (t0, tn) in TG])
                for ct in range(8):
                    wts = _ws.get()
                    for hh in range(2):
                        j = ct * 2 + hh
                        for (t0, tn) in TG:
                            acc, b_acc = stg.next()
                            g3 = _gs.get()
                            for br in range(3):
                                wb, b_wb = wts[br]
                                xT_, b_xT_, kofs = srcs[br]
                                pt, b_pt = psF.next()
                                mm_F(pt, b_pt, wb, b_wb, 8, hh * 128, xT_, b_xT_, t0, tn, kofs=kofs)
                                gt, b_gt = g3[br]
                                if br == 0:
                                    V(lambda e, acc=acc, pt=pt, gt=gt: e.tensor_tensor(out=acc[:], in0=pt[:], in1=gt[:], op=ALU.mult), [b_pt, b_gt], [b_acc])
                                else:
                                    t2, b_t2 = stg.next()
                                    V(lambda e, t2=t2, pt=pt, gt=gt: e.tensor_tensor(out=t2[:], in0=pt[:], in1=gt[:], op=ALU.mult), [b_pt, b_gt], [b_t2])
                                    if br == 1:
                                        G(lambda e, acc=acc, t2=t2: e.tensor_tensor(out=acc[:], in0=acc[:], in1=t2[:], op=ALU.add), [b_acc, b_t2], [b_acc])
                                    else:
                                        mb, b_mb = stgb.next()
                                        G(lambda e, acc=acc, t2=t2, mb=mb: e.tensor_tensor(out=mb[:], in0=acc[:], in1=t2[:], op=ALU.add), [b_acc, b_t2], [b_mb])
                                        DM(MIXT[j, :, t0:t0 + tn], mb[:], r=[b_mb])
                scr_barrier(["MIXT"])

        def phase_wout(l):
            hT = cur["hT"]
            with PhaseCtx("wout"):
                pc = ProjCtx()
                DM(hT[:], MIXT.rearrange("c p t -> p c t"), r=[b_scr["MIXT"]], w=[b_hT])
                _ws = pc.stream([(lambda ct=ct: pc.load_w(w_out[l, :, ct * 256:(ct + 1) * 256].rearrange("(k p) n -> p k n", p=128), 16, 256))
                                 for ct in range(8)])
                for ct in range(8):
                    wb, b_wb = _ws.get()
                    for tile in range(NT):
                        pt, b_pt = psF.next()
                        mm_T(pt, b_pt, wb, b_wb, 16, 256, hT, b_hT, tile)
                        pc.store_f32(MO[tile * 128:(tile + 1) * 128, ct * 256:(ct + 1) * 256], pt, b_pt, 256)
                all_dma_into(b_X["MO"])

        def phase_ffn(l):
            hT = cur["hT"]
            for qq in range(4):
                with PhaseCtx("ffn%d" % qq):
                    pc = ProjCtx(nstg=3, nstgb=0)
                    gT = fw.sb("gT", [128, 11, T], BF16); b_gT = fw.buf()
                    prevr = Ring(fw, "prevr", [128, 256], F32, 8)
                    def _ldup(hc):
                        ws, b_ws = pc.wst.next()
                        DM(ws[:, :, 0:128], w_up[l, :, hc * 128:(hc + 1) * 128].rearrange("(k p) n -> p k n", p=128), w=[b_ws])
                        ev = fw.dma(lambda e, ws=ws, hc=hc: e.dma_start(
                            out=ws[:, :, 128:256], in_=w_up[l, :, FFN_H + hc * 128:FFN_H + (hc + 1) * 128].rearrange("(k p) n -> p k n", p=128)), [], [])
                        b_ws.w[ev[0]] = ev[1]
                        wb, b_wb = pc.wbf.next()
                        G(lambda e, wb=wb, ws=ws: e.tensor_copy(out=wb[:], in_=ws[:]), [b_ws], [b_wb])
                        return wb, b_wb
                    _ws = pc.stream([(lambda hc=qq * 11 + jj: _ldup(hc)) for jj in range(11)] +
                                    [(lambda ct=ct: pc.load_w(w_dn[l, qq * 1408:(qq + 1) * 1408, ct * 256:(ct + 1) * 256].rearrange("(k p) n -> p k n", p=128), 11, 256))
                                     for ct in range(8)])
                    for jj in range(11):
                        hc = qq * 11 + jj
                        wb, b_wb = _ws.get()
                        for (t0, tn) in TG:
                            pg, b_pg = psF.next()
                            mm_F(pg, b_pg, wb, b_wb, 16, 0, hT, b_hT, t0, tn)
                            pu, b_pu = psF.next()
                            mm_F(pu, b_pu, wb, b_wb, 16, 128, hT, b_hT, t0, tn)
                            sg, b_sg = pc.stg.next()
                            A(lambda e, sg=sg, pg=pg: e.activation(out=sg[:], in_=pg[:], func=AF.Silu), [b_pg], [b_sg])
                            V(lambda e, sg=sg, pu=pu, jj=jj, t0=t0, tn=tn: e.tensor_tensor(out=gT[:, jj, t0:t0 + tn], in0=pu[:], in1=sg[:], op=ALU.mult),
                              [b_pu, b_sg], [b_gT])
                    def _ldpv(ct, tile):
                        pv, b_pv = prevr.next()
                        DM(pv[:], MO[tile * 128:(tile + 1) * 128, ct * 256:(ct + 1) * 256], r=[b_X["MO"]], w=[b_pv])
                        return pv, b_pv
                    _pvl = [(lambda ct=ct, tile=tile: _ldpv(ct, tile)) for ct in range(8) for tile in range(NT)] if qq > 0 else []
                    _pvq = []
                    _pvi = [0]

                    def _pv_fill(depth_):
                        while _pvi[0] < len(_pvl) and len(_pvq) < depth_:
                            _pvq.append(_pvl[_pvi[0]]())
                            _pvi[0] += 1
                    for ct in range(8):
                        wb, b_wb = _ws.get()
                        for tile in range(NT):
                            if qq > 0:
                                _pv_fill(5)
                            pt, b_pt = psF.next()
                            mm_T(pt, b_pt, wb, b_wb, 11, 256, gT, b_gT, tile)
                            dst = MO[tile * 128:(tile + 1) * 128, ct * 256:(ct + 1) * 256]
                            if qq == 0:
                                pc.store_f32(dst, pt, b_pt, 256)
                            else:
                                pv, b_pv = _pvq.pop(0)
                                V(lambda e, pv=pv, pt=pt: e.tensor_tensor(out=pv[:], in0=pt[:, 0:256], in1=pv[:], op=ALU.add), [b_pt, b_pv], [b_pv])
                                DM(dst, pv[:], r=[b_pv])
                    all_dma_into(b_X["MO"])

        def want(name):
            return stop_after is None or True

        gst = ExitStack()
        fw.stack = gst
        cur["hT"] = fw.sb("hT", [128, 16, T], BF16)
        fw.stack = st
        if stop_after != "init":
            phase_cs()
            phase_mods(0)
        if stop_after not in ("init", "mods"):
            norm_pass(0, xin, b_X["xin"], None, None, None, None, None, None, 0, 1, 0, n_w[0])
        done = False
        for l in range(depth if stop_after not in ("init", "mods", "np1") else 0):
            phase_win(l)
            if stop_after == "win":
                done = True
                break
            gst.close()
            phase_na(l)
            phase_gqa(l)
            phase_ssd(l)
            gst = ExitStack()
            fw.stack = gst
            cur["hT"] = fw.sb("hT", [128, 16, T], BF16)
            fw.stack = st
            if stop_after == "mix":
                done = True
                break
            phase_branch(l)
            phase_wout(l)
            x_src, bx_src = (xin, b_X["xin"]) if l == 0 else (XB, b_X["XB"])
            norm_pass(l, x_src, bx_src, MO, b_X["MO"], 2, n_w[1], XA, b_X["XA"], l, 4, 3, n_w[2])
            if stop_after == "np2":
                done = True
                break
            phase_ffn(l)
            last = (l == depth - 1)
            if last:
                norm_pass(l, XA, b_X["XA"], MO, b_X["MO"], 5, n_w[3], y_out, fw.buf("yout"), None, None, None, None)
            else:
                norm_pass(l, XA, b_X["XA"], MO, b_X["MO"], 5, n_w[3], XB, b_X["XB"], l + 1, 1, 0, n_w[0])
        gst.close()
        with PhaseCtx("fin"):
            pass
    return nc


_CACHE = {}


def kernel(**inp):
    f32 = np.float32
    g = lambda k: np.ascontiguousarray(np.asarray(inp[k], dtype=f32))
    if "nc" not in _CACHE:
        _CACHE["nc"] = build_program(DEPTH)
        _CACHE["consts"] = host_consts()
    nc = _CACHE["nc"]
    kc = _CACHE["consts"]
    xp = g("x_prompt"); xs = g("x_sample")
    shared = {
        "ada_w": g("ada_w"), "ada_b": g("ada_b"),
        "norm_mix_pre": g("norm_mix_pre"), "norm_mix_post": g("norm_mix_post"),
        "norm_ffn_pre": g("norm_ffn_pre"), "norm_ffn_post": g("norm_ffn_post"),
        "w_in": g("w_in"), "gate_b": g("gate_b"), "na_rpb": g("na_rpb").reshape(DEPTH, -1),
        "gqa_q_norm": g("gqa_q_norm"), "gqa_k_norm": g("gqa_k_norm"),
        "ssd_conv_w": g("ssd_conv_w"), "ssd_conv_b": g("ssd_conv_b"),
        "ssd_dt_bias": g("ssd_dt_bias").reshape(DEPTH, 32), "ssd_a_log": g("ssd_a_log").reshape(DEPTH, 32),
        "ssd_d": g("ssd_d").reshape(DEPTH, 32), "ssd_norm_w": g("ssd_norm_w"),
        "w_branch": g("w_branch"), "w_out": g("w_out"), "ffn_w_up": g("ffn_w_up"), "ffn_w_down": g("ffn_w_down"),
        "k_ident": kc["ident"], "k_ones": kc["ones"], "k_ssdm": kc["ssdm"], "k_ropec": kc["ropec"], "k_ropes": kc["ropes"],
        "k_ropeR": kc["ropeR"], "k_flipJ": kc["flipJ"], "k_namask": kc["namask"],
    }
    cnk, cnv, cgk, cgv, sst = g("cache_na_k"), g("cache_na_v"), g("cache_gqa_k"), g("cache_gqa_v"), g("state_ssd")
    cc, cctx = g("c"), g("c_ctx")
    in_maps = []
    for c in range(8):
        b = c // 2
        m = dict(shared)
        m["xin"] = np.ascontiguousarray(np.concatenate([xp[2 * c], xp[2 * c + 1], xs[b]], axis=0))
        m["cv"] = np.ascontiguousarray(np.stack([cctx, cc[b]], axis=0))
        m["c_nak"] = cnk[b]; m["c_nav"] = cnv[b]; m["c_gk"] = cgk[b]; m["c_gv"] = cgv[b]
        m["c_ssd"] = np.ascontiguousarray(sst[b].reshape(DEPTH, 2, 1024, 128))
        in_maps.append(m)
    res = run_bass_kernel_spmd(nc, in_maps, core_ids=list(range(8)))
    R = res.results
    y_prompt = np.empty((16, 256, D), f32)
    y_sample = np.empty((4, 2048, D), f32)
    nak = np.empty((16, DEPTH, 256, 8, 128), f32)
    nav = np.empty((16, DEPTH, 256, 8, 128), f32)
    gk = np.empty((16, DEPTH, 256, 2, 128), f32)
    gv = np.empty((16, DEPTH, 256, 2, 128), f32)
    ssd = np.empty((16, DEPTH, 2, 16, 64, 128), f32)
    for c in range(8):
        r = R[c]
        y = np.asarray(r["y_out"])
        for s in range(2):
            bi = 2 * c + s
            y_prompt[bi] = y[s * 256:(s + 1) * 256]
            nak[bi] = np.asarray(r["o_nak"])[:, s * 256:(s + 1) * 256].reshape(DEPTH, 256, 8, 128)
            nav[bi] = np.asarray(r["o_nav"])[:, s * 256:(s + 1) * 256].reshape(DEPTH, 256, 8, 128)
            gk[bi] = np.asarray(r["o_gk"])[:, s * 256:(s + 1) * 256].reshape(DEPTH, 256, 2, 128)
            gv[bi] = np.asarray(r["o_gv"])[:, s * 256:(s + 1) * 256].reshape(DEPTH, 256, 2, 128)
            ssd[bi] = np.asarray(r["o_ssd"])[s].reshape(DEPTH, 2, 16, 64, 128)
        if c % 2 == 0:
            y_sample[c // 2] = y[512:]
    return (y_prompt, y_sample, nak, nav, gk, gv, ssd)
```

```python
import math
from contextlib import ExitStack
import numpy as np
import concourse.bass as bass
import concourse.mybir as mybir
from concourse.bass_utils import run_bass_kernel_spmd
from concourse.bass_types import AP

F32 = mybir.dt.float32
BF16 = mybir.dt.bfloat16
AF = mybir.ActivationFunctionType
ALU = mybir.AluOpType
AX = mybir.AxisListType

D = 2048
DEPTH = 4
T = 2560
NT = 20
PT = 512
SEQS = [(0, 256, 0), (256, 256, 0), (512, 2048, 1)]
N_IN = 13344
FFN_H = 5632
EPS = 1e-6
NEG = -1e30


class Buf:
    __slots__ = ("name", "w", "r")

    def __init__(self, name):
        self.name = name
        self.w = {}
        self.r = {}


class Eng:
    def __init__(self, name, sem):
        self.name = name
        self.sem = sem
        self.count = 0
        self.ops = []
        self.seen = {}


class Fw:
    NDMA = 56

    def __init__(self, nc, stack):
        self.nc = nc
        self.stack = stack
        self.sems = []
        self.eng = {}
        for n in ("pe", "act", "dve", "pool", "sp"):
            s = stack.enter_context(nc.semaphore("prog_" + n))
            self.sems.append(s)
            self.eng[n] = Eng(n, len(self.sems) - 1)
        self.dma_slots = []
        for i in range(self.NDMA):
            s = stack.enter_context(nc.semaphore("dma_%d" % i))
            self.sems.append(s)
            self.dma_slots.append([len(self.sems) - 1, 0])
        self.dma_rr = 0
        self.nbuf = 0

    def buf(self, name=None):
        self.nbuf += 1
        return Buf(name or "b%d" % self.nbuf)

    def sb(self, name, shape, dt):
        self.nalloc = getattr(self, "nalloc", 0) + 1
        return self.stack.enter_context(self.nc.sbuf_tensor("%s_%d" % (name, self.nalloc), list(shape), dt))

    def ps(self, name, shape, dt=F32):
        self.nalloc = getattr(self, "nalloc", 0) + 1
        return self.stack.enter_context(self.nc.psum_tensor("%s_%d" % (name, self.nalloc), list(shape), dt))

    def _waits(self, e, reads, writes):
        need = {}
        for b in reads:
            for s, v in b.w.items():
                if need.get(s, 0) < v:
                    need[s] = v
        for b in writes:
            for s, v in b.w.items():
                if need.get(s, 0) < v:
                    need[s] = v
            for s, v in b.r.items():
                if need.get(s, 0) < v:
                    need[s] = v
        out = []
        for s, v in need.items():
            if e.seen.get(s, 0) < v:
                e.seen[s] = v
                out.append((s, v))
        return out

    def op(self, en, fn, reads=(), writes=()):
        e = self.eng[en]
        waits = self._waits(e, reads, writes)
        if en == "pe":
            waits = [(s, v) for (s, v) in waits if s != e.sem]
        e.count += 1
        ev = (e.sem, e.count)
        for b in reads:
            if b.r.get(ev[0], 0) < ev[1]:
                b.r[ev[0]] = ev[1]
        for b in writes:
            b.w = {ev[0]: ev[1]}
            b.r = {}
        e.ops.append((waits, fn, (e.sem, 1)))
        return ev

    def dma(self, fn, reads=(), writes=(), q="sp"):
        e = self.eng[q]
        slot = self.dma_slots[self.dma_rr]
        self.dma_rr = (self.dma_rr + 1) % self.NDMA
        waits = self._waits(e, reads, writes)
        if slot[1] > 0 and e.seen.get(slot[0], 0) < slot[1]:
            e.seen[slot[0]] = slot[1]
            waits.append((slot[0], slot[1]))
        slot[1] += 16
        ev = (slot[0], slot[1])
        for b in reads:
            if b.r.get(ev[0], 0) < ev[1]:
                b.r[ev[0]] = ev[1]
        for b in writes:
            b.w = {ev[0]: ev[1]}
            b.r = {}
        e.ops.append((waits, fn, (slot[0], 16)))
        return ev

    def final_wait(self, bufs):
        e = self.eng["sp"]
        waits = self._waits(e, bufs, bufs)
        e.ops.append((waits, None, None))

    def phase_end(self):
        e = self.eng["sp"]
        waits = []
        for sl in self.dma_slots:
            if sl[1] > 0 and e.seen.get(sl[0], 0) < sl[1]:
                e.seen[sl[0]] = sl[1]
                waits.append((sl[0], sl[1]))
        e.ops.append((waits, None, None))

    def emit(self):
        nc = self.nc
        handles = {"pe": "tensor", "act": "scalar", "dve": "vector", "pool": "gpsimd", "sp": "sync"}
        sems = self.sems
        with nc.Block() as block:
            for n, e in self.eng.items():
                if not e.ops:
                    continue

                def body(h, e=e):
                    for (waits, fn, inc) in e.ops:
                        for (s, v) in waits:
                            h.wait_ge(sems[s], v)
                        if fn is not None:
                            fn(h).then_inc(sems[inc[0]], inc[1])
                getattr(block, handles[n])(body)
        for e in self.eng.values():
            e.ops = []


class Ring:
    def __init__(self, fw, name, shape, dt, n, psum=False):
        self.items = []
        for i in range(n):
            t = (fw.ps if psum else fw.sb)("%s%d" % (name, i), shape, dt)
            self.items.append((t, fw.buf("%s%d" % (name, i))))
        self.i = 0

    def next(self):
        it = self.items[self.i]
        self.i = (self.i + 1) % len(self.items)
        return it


def host_consts():
    c = {}
    c["ident"] = np.eye(128, dtype=np.float32)
    c["ones"] = np.ones((128, 128), dtype=np.float32)
    k = np.arange(128)[:, None]
    j = np.arange(128)[None, :]
    c["ssdm"] = np.stack([
        (k > j), (k <= j), (k <= j),
        (k < j), (k >= j), (k >= j),
    ]).astype(np.float32)
    nf = 32
    inv = 10000.0 ** (-np.arange(nf, dtype=np.float64) / nf)
    t = np.arange(2048)
    d = np.arange(128)
    pos = np.where(d[:, None] < 64, (t // 64)[None, :], (t % 64)[None, :]).astype(np.float64)
    fr = inv[(d % 64) % 32][:, None]
    ang = (pos.astype(np.float32) * fr.astype(np.float32)).astype(np.float32)
    c["ropec"] = np.cos(ang).astype(np.float32)
    c["ropes"] = np.sin(ang).astype(np.float32)
    R = np.zeros((128, 128), dtype=np.float32)
    for m in range(128):
        if (m % 64) < 32:
            R[m + 32, m] = -1.0
        else:
            R[m - 32, m] = 1.0
    c["ropeR"] = R
    J = np.zeros((128, 128), dtype=np.float32)
    for m in range(128):
        J[(m // 64) * 64 + 63 - (m % 64), m] = 1.0
    c["flipJ"] = J
    masks = np.zeros((5, 128, 640), dtype=np.float32)
    a0s = []
    for ti, jt in enumerate([0, 1, 5, 14, 15]):
        tw = min(max(jt - 2, 0), 11)
        for q in range(128):
            rq = 2 * jt + q // 64
            cq = q % 64
            rs = min(max(rq - 4, 0), 24)
            cs = min(max(cq - 8, 0), 48)
            for rl in range(10):
                rk = 2 * tw + rl
                okr = (rk >= rs) and (rk < rs + 8)
                for ck in range(64):
                    ok = okr and (ck >= cs) and (ck < cs + 16)
                    masks[ti, q, rl * 64 + ck] = 0.0 if ok else NEG
    c["namask"] = masks
    return c


def na_tile_type(jt):
    if jt == 0:
        return 0
    if jt == 1:
        return 1
    if jt == 14:
        return 3
    if jt == 15:
        return 4
    return 2


def build_program(depth=DEPTH, debug=False, stop_after=None):
    nc = bass.Bass("TRN2", target_bir_lowering=False)

    def din(name, shape, dt=F32):
        return nc.dram_tensor(name, list(shape), dt, kind="ExternalInput").ap()

    def dout(name, shape, dt=F32):
        return nc.dram_tensor(name, list(shape), dt, kind="ExternalOutput").ap()

    def dscr(name, shape, dt=F32):
        return nc.dram_tensor(name, list(shape), dt, kind=("ExternalOutput" if debug else "Internal")).ap()

    L_ = depth
    xin = din("xin", [T, D])
    cv = din("cv", [2, D])
    c_nak = din("c_nak", [L_, 512, 8, 128])
    c_nav = din("c_nav", [L_, 512, 8, 128])
    c_gk = din("c_gk", [L_, 512, 2, 128])
    c_gv = din("c_gv", [L_, 512, 2, 128])
    c_ssd = din("c_ssd", [L_, 2, 1024, 128])
    ada_w = din("ada_w", [L_, D, 6 * D])
    ada_b = din("ada_b", [L_, 6 * D])
    n_w = [din(n, [L_, D]) for n in ("norm_mix_pre", "norm_mix_post", "norm_ffn_pre", "norm_ffn_post")]
    w_in = din("w_in", [L_, D, N_IN])
    gate_b = din("gate_b", [L_, 3 * D])
    na_rpb = din("na_rpb", [L_, 8 * 15 * 31])
    qn_w = din("gqa_q_norm", [L_, 128])
    kn_w = din("gqa_k_norm", [L_, 128])
    conv_w = din("ssd_conv_w", [L_, 3, 1536])
    conv_b = din("ssd_conv_b", [L_, 1536])
    dt_bias = din("ssd_dt_bias", [L_, 32])
    a_log = din("ssd_a_log", [L_, 32])
    ssd_d = din("ssd_d", [L_, 32])
    ssd_nw = din("ssd_norm_w", [L_, 1024])
    w_br = din("w_branch", [L_, 3, 1024, D])
    w_out = din("w_out", [L_, D, D])
    w_up = din("ffn_w_up", [L_, D, 2 * FFN_H])
    w_dn = din("ffn_w_down", [L_, FFN_H, D])
    k_ident = din("k_ident", [128, 128])
    k_ones = din("k_ones", [128, 128])
    k_ssdm = din("k_ssdm", [6, 128, 128])
    k_ropec = din("k_ropec", [128, 2048])
    k_ropes = din("k_ropes", [128, 2048])
    k_ropeR = din("k_ropeR", [128, 128])
    k_flipJ = din("k_flipJ", [128, 128])
    k_namask = din("k_namask", [5, 128, 640])

    y_out = dout("y_out", [T, D])
    o_nak = dout("o_nak", [L_, PT, 1024])
    o_nav = dout("o_nav", [L_, PT, 1024])
    o_gk = dout("o_gk", [L_, PT, 256])
    o_gv = dout("o_gv", [L_, PT, 256])
    o_ssd = dout("o_ssd", [2, L_, 2, 1024, 128])

    MODS = dscr("MODS", [L_, 2, 6 * D])
    XA = dscr("XA", [T, D])
    XB = dscr("XB", [T, D])
    MO = dscr("MO", [T, D])
    QAT = dscr("QAT", [8, 128, T], BF16)
    KAT = dscr("KAT", [8, 128, T], BF16)
    VAH = dscr("VAH", [8, T, 128], BF16)
    QCT = dscr("QCT", [8, 128, T], BF16)
    KCT = dscr("KCT", [2, 128, T], BF16)
    VCH = dscr("VCH", [2, T, 128], BF16)
    SZ = dscr("SZ", [T, 1024], BF16)
    XBCT = dscr("XBCT", [12, 128, T])
    DTS = dscr("DTS", [T, 32])
    GT = dscr("GT", [48, 128, T], BF16)
    OXT = [dscr("OXT%d" % i, [8, 128, T], BF16) for i in range(3)]
    MIXT = dscr("MIXT", [16, 128, T], BF16)
    RPBP = dscr("RPBP", [8 * 15 * 31 + 4096])

    st = ExitStack()
    with st:
        fw = Fw(nc, st)

        def V(fn, r=(), w=()):
            return fw.op("dve", fn, r, w)

        def A(fn, r=(), w=()):
            return fw.op("act", fn, r, w)

        def G(fn, r=(), w=()):
            return fw.op("pool", fn, r, w)

        def P(fn, r=(), w=()):
            return fw.op("pe", fn, r, w)

        def DM(out, in_, r=(), w=(), q="sp"):
            return fw.dma(lambda e: e.dma_start(out=out, in_=in_), r, w, q=q)

        evac_rr = [0]

        def EVC(dst, src, r, w):
            evac_rr[0] ^= 1
            if evac_rr[0]:
                return A(lambda e: e.activation(out=dst, in_=src, func=AF.Copy), r, w)
            return V(lambda e: e.tensor_copy(out=dst, in_=src), r, w)

        class PhaseCtx:
            def __init__(self, name):
                self.name = name

            def __enter__(self):
                self.old = fw.stack
                self.ps = ExitStack()
                self.ps.__enter__()
                fw.stack = self.ps
                return self

            def __exit__(self, *a):
                if a[0] is None:
                    fw.phase_end()
                    fw.emit()
                self.ps.__exit__(*a)
                fw.stack = self.old
                return False

        def all_dma_into(b):
            for sl in fw.dma_slots:
                if sl[1] > 0:
                    b.w[sl[0]] = max(b.w.get(sl[0], 0), sl[1])
            b.r = {}

        ident = fw.sb("ident", [128, 128], F32); b_const = fw.buf("const")
        identb = fw.sb("identb", [128, 128], BF16)
        ones = fw.sb("ones", [128, 128], F32)
        onesb = fw.sb("onesb", [128, 128], BF16)
        ssdm = fw.sb("ssdm", [128, 6, 128], F32)
        ropeR = fw.sb("ropeR", [128, 128], F32)
        flipJ = fw.sb("flipJ", [128, 128], F32)
        epsT = fw.sb("epsT", [128, 1], F32)
        sq_junk = fw.sb("sq_junk", [128, 2048], BF16); b_junk = fw.buf()
        b_c2 = fw.buf("const2")
        psF = Ring(fw, "psF", [128, 512], F32, 6, psum=True)
        psB = Ring(fw, "psB", [128, 1024], BF16, 2, psum=True)
        stat = Ring(fw, "stat", [128, 4], F32, 6)
        CONST = [b_const, b_c2]
        b_hT = fw.buf("hT")
        cur = {"hT": None}
        b_MODS = fw.buf("MODS")
        b_X = {"xin": fw.buf("xin"), "XA": fw.buf("XA"), "XB": fw.buf("XB"), "MO": fw.buf("MO")}
        b_scr = {n: fw.buf(n) for n in ("QAT", "KAT", "VAH", "QCT", "KCT", "VCH", "SZ", "XBCT", "DTS", "GT",
                                        "OXT0", "OXT1", "OXT2", "MIXT", "RPBP")}

        def scr_barrier(names):
            for n in names:
                all_dma_into(b_scr[n])

        with PhaseCtx("init"):
            DM(ident[:], k_ident, w=[b_const])
            DM(ones[:], k_ones, w=[b_const])
            DM(ssdm[:], k_ssdm.rearrange("a p n -> p a n"), w=[b_const])
            DM(ropeR[:], k_ropeR, w=[b_const])
            DM(flipJ[:], k_flipJ, w=[b_const])
            all_dma_into(b_const)
            V(lambda e: e.tensor_copy(out=identb[:], in_=ident[:]), [b_const], [b_c2])
            V(lambda e: e.tensor_copy(out=onesb[:], in_=ones[:]), [b_const], [b_c2])
            V(lambda e: e.memset(epsT[:], EPS), [], [b_c2])

        def load_T(dst_ap, n, srcs, tmpr):
            tmp, b_tmp = tmpr.next()
            for (r0, nr, ap) in srcs:
                DM(tmp[r0:r0 + nr, :], ap, w=[b_tmp])
            pt, b_pt = psF.next()
            P(lambda e: e.transpose(out=pt[:, 0:n], in_=tmp[0:n, :], identity=ident[0:n, 0:n]), [b_tmp] + CONST, [b_pt])
            b_d = fw.buf()
            V(lambda e: e.tensor_copy(out=dst_ap, in_=pt[:, 0:n]), [b_pt], [b_d])
            return b_d

        def rstd_of(src_ap, b_src, n):
            s_, b_s = stat.next()
            A(lambda e: e.activation(out=sq_junk[:, 0:n], in_=src_ap, func=AF.Square, accum_out=s_[:, 0:1]),
              [b_src], [b_junk, b_s])
            A(lambda e: e.activation(out=s_[:, 1:2], in_=s_[:, 0:1], func=AF.Sqrt, scale=1.0 / n, bias=epsT[:]),
              [b_s] + CONST, [b_s])
            V(lambda e: e.reciprocal(out=s_[:, 2:3], in_=s_[:, 1:2]), [b_s], [b_s])
            return s_[:, 2:3], b_s

        csT = fw.sb("csT", [128, 16, 2], F32)
        b_cs = fw.buf("cs")

        def phase_cs():
            with PhaseCtx("cs"):
                tmpr = Ring(fw, "tmpT", [128, 128], F32, 1)
                cs32 = fw.sb("cs32", [128, 32], F32)
                b_c = load_T(cs32[:], 32, [(0, 32, cv.rearrange("r (k p) -> (r k) p", p=128))], tmpr)
                A(lambda e: e.activation(out=cs32[:], in_=cs32[:], func=AF.Silu), [b_c], [b_c])
                V(lambda e: e.tensor_copy(out=csT[:], in_=cs32[:].rearrange("p (r k) -> p k r", r=2)), [b_c], [b_cs])

        def mods_steps(l):
            wst = Ring(fw, "mwst", [128, 16, 256], F32, 2)
            modst = Ring(fw, "modst", [2, 256], F32, 3)
            adab = Ring(fw, "adab", [2, 256], F32, 3)

            def load(ct):
                ws, b_ws = wst.next()
                DM(ws[:], ada_w[l, :, ct * 256:(ct + 1) * 256].rearrange("(k p) n -> p k n", p=128), w=[b_ws])
                ab, b_ab = adab.next()
                DM(ab[:], ada_b[l:l + 1, ct * 256:(ct + 1) * 256].partition_broadcast(2).rearrange("p a n -> p (a n)"), w=[b_ab])
                return ws, b_ws, ab, b_ab
            nxt = load(0)
            for ct in range(48):
                ws, b_ws, ab, b_ab = nxt
                nxt = load(ct + 1) if ct + 1 < 48 else None
                pt, b_pt = psF.next()
                for k in range(16):
                    P(lambda e, k=k, ws=ws, pt=pt: e.matmul(pt[0:2, 0:256], lhsT=csT[:, k, :], rhs=ws[:, k, :],
                                                             start=(k == 0), stop=(k == 15)),
                      [b_cs, b_ws], [b_pt])
                ms, b_ms = modst.next()
                V(lambda e, ms=ms, pt=pt, ab=ab: e.tensor_tensor(out=ms[:], in0=pt[0:2, 0:256], in1=ab[:], op=ALU.add),
                  [b_pt, b_ab], [b_ms])
                DM(MODS[l, :, ct * 256:(ct + 1) * 256], ms[:], r=[b_ms])
                yield
            all_dma_into(b_MODS)

        def phase_mods(l):
            with PhaseCtx("mods"):
                for _ in mods_steps(l):
                    pass

        def norm_pass(l, x_src, bx_src, add_src, b_add, gate_seg, post_nw, x_dst, bx_dst,
                      pre_l, sc_seg, sh_seg, pre_nw):
            hT = cur["hT"]
            with PhaseCtx("norm"):
                bcs = {k_: (fw.sb("bc_" + k_, [128, 2048], F32), fw.buf()) for k_ in ("g", "A", "sh", "tmp")}
                xg = Ring(fw, "xg", [128, 2048], F32, 8)
                hb = Ring(fw, "hb", [128, 2048], BF16, 2)

                def _ld_tile(tile):
                    rows_ = slice(tile * 128, (tile + 1) * 128)
                    x_t, b_x = xg.next()
                    DM(x_t[:], x_src[rows_, :], r=[bx_src], w=[b_x])
                    if add_src is not None:
                        a_t, b_a = xg.next()
                        DM(a_t[:], add_src[rows_, :], r=[b_add], w=[b_a])
                        return x_t, b_x, a_t, b_a
                    return x_t, b_x, None, None
                _pend = {}

                def load_bc(src_row_ap, rbufs=(), which="tmp"):
                    t_, b_ = bcs[which]
                    DM(t_[:], src_row_ap.partition_broadcast(128).rearrange("p a n -> p (a n)"), r=list(rbufs), w=[b_])
                    return t_, b_

                import os as _os
                _lim = int(_os.environ.get("DBG_NTILES", "99"))
                _var = _os.environ.get("DBG_VAR", "")
                for r in range(2):
                    tiles = range(0, 4) if r == 0 else range(4, NT)
                    tiles = [t_ for t_ in tiles if t_ < _lim]
                    if not tiles:
                        continue
                    if add_src is not None:
                        g_t, b_g = load_bc(MODS[l, r:r + 1, gate_seg * D:(gate_seg + 1) * D], [b_MODS], "g")
                        nw_t, b_nw = load_bc(post_nw[l:l + 1, :])
                        V(lambda e, g_t=g_t, nw_t=nw_t: e.tensor_tensor(out=g_t[:], in0=g_t[:], in1=nw_t[:], op=ALU.mult), [b_g, b_nw], [b_g])
                    if pre_l is not None:
                        A_t, b_A = load_bc(MODS[pre_l, r:r + 1, sc_seg * D:(sc_seg + 1) * D], [b_MODS], "A")
                        nw2, b_nw2 = load_bc(pre_nw[pre_l:pre_l + 1, :])
                        V(lambda e, A_t=A_t, nw2=nw2: e.scalar_tensor_tensor(out=A_t[:], in0=A_t[:], scalar=1.0, in1=nw2[:], op0=ALU.add, op1=ALU.mult),
                          [b_A, b_nw2], [b_A])
                        sh_t, b_sh = load_bc(MODS[pre_l, r:r + 1, sh_seg * D:(sh_seg + 1) * D], [b_MODS], "sh")
                    tiles = list(tiles)
                    for ti_, tile in enumerate(tiles):
                        rows = slice(tile * 128, (tile + 1) * 128)
                        if tile not in _pend:
                            _pend[tile] = _ld_tile(tile)
                        if ti_ + 1 < len(tiles) and tiles[ti_ + 1] not in _pend:
                            _pend[tiles[ti_ + 1]] = _ld_tile(tiles[ti_ + 1])
                        x_t, b_x, a_t, b_a = _pend.pop(tile)
                        if add_src is not None:
                            r_ap, b_r = rstd_of(a_t[:], b_a, 2048)
                            V(lambda e, a_t=a_t, r_ap=r_ap, g_t=g_t: e.scalar_tensor_tensor(
                                out=a_t[:], in0=a_t[:], scalar=r_ap, in1=g_t[:], op0=ALU.mult, op1=ALU.mult),
                              [b_a, b_r, b_g], [b_a])
                            G(lambda e, a_t=a_t, x_t=x_t: e.tensor_tensor(out=x_t[:], in0=x_t[:], in1=a_t[:], op=ALU.add),
                              [b_a, b_x], [b_x])
                            DM(x_dst[rows, :], x_t[:], r=[b_x])
                        if pre_l is not None:
                            r_ap, b_r = rstd_of(x_t[:], b_x, 2048)
                            tmp, b_tmp = xg.next()
                            V(lambda e, tmp=tmp, x_t=x_t, r_ap=r_ap, A_t=A_t: e.scalar_tensor_tensor(
                                out=tmp[:], in0=x_t[:], scalar=r_ap, in1=A_t[:], op0=ALU.mult, op1=ALU.mult),
                              [b_x, b_r, b_A], [b_tmp])
                            h_, b_h = hb.next()
                            G(lambda e, tmp=tmp, h_=h_, sh_t=sh_t: e.tensor_tensor(out=h_[:], in0=tmp[:], in1=sh_t[:], op=ALU.add), [b_tmp, b_sh], [b_h])
                            for half in range(2):
                                pb, b_pb = psB.next()
                                for kk in range(8):
                                    k = half * 8 + kk
                                    P(lambda e, k=k, kk=kk, pb=pb, h_=h_: e.transpose(out=pb[:, kk * 128:(kk + 1) * 128],
                                                                                      in_=h_[:, k * 128:(k + 1) * 128], identity=identb[:]),
                                      [b_h] + CONST, [b_pb])
                                EVC(hT[:, half * 8:(half + 1) * 8, tile * 128:(tile + 1) * 128],
                                    pb[:].rearrange("p (k t) -> p k t", k=8), [b_pb], [b_hT])
                if add_src is not None:
                    all_dma_into(bx_dst)

        def mm_F(pt, b_pt, wb, b_wb, kc, c0, xT, b_xT, t0, tn, kofs=0):
            for k in range(kc):
                P(lambda e, k=k: e.matmul(pt[:, 0:tn], lhsT=wb[:, k, c0:c0 + 128], rhs=xT[:, kofs + k, t0:t0 + tn],
                                          start=(k == 0), stop=(k == kc - 1)),
                  [b_wb, b_xT], [b_pt])

        def mm_T(pt, b_pt, wb, b_wb, kc, ncols, xT, b_xT, tile, kofs=0):
            for k in range(kc):
                P(lambda e, k=k: e.matmul(pt[:, 0:ncols], lhsT=xT[:, kofs + k, tile * 128:(tile + 1) * 128], rhs=wb[:, k, 0:ncols],
                                          start=(k == 0), stop=(k == kc - 1)),
                  [b_wb, b_xT], [b_pt])

        TG = [(g * 512, 512) for g in range(5)]

        class ProjCtx:
            def __init__(self, kcmax=16, nstg=4, nstgb=4):
                self.wst = Ring(fw, "wst", [128, kcmax, 256], F32, 2)
                self.wbf = Ring(fw, "wbf", [128, kcmax, 256], BF16, 2)
                self.stg = Ring(fw, "stg", [128, 512], F32, nstg)
                self.stgb = Ring(fw, "stgb", [128, 512], BF16, nstgb) if nstgb else None

            def load_w(self, src_ap_pkn, kc, ncols):
                ws, b_ws = self.wst.next()
                DM(ws[:, 0:kc, 0:ncols], src_ap_pkn, w=[b_ws])
                wb, b_wb = self.wbf.next()
                G(lambda e: e.tensor_copy(out=wb[:, 0:kc, 0:ncols], in_=ws[:, 0:kc, 0:ncols]), [b_ws], [b_wb])
                return wb, b_wb

            def stream(self, loaders):
                pc_ = self

                class _S:
                    def __init__(s_):
                        s_.i = 0
                        s_.pending = loaders[0]() if loaders else None

                    def get(s_):
                        cur = s_.pending
                        s_.i += 1
                        s_.pending = loaders[s_.i]() if s_.i < len(loaders) else None
                        return cur
                return _S()

            def store_bf(self, dst_ap, pt, b_pt, n, func=None, bias=None, bias_b=(), view=None):
                s_, b_s = self.stgb.next()
                if func is None:
                    EVC(s_[:, 0:n], pt[:, 0:n], [b_pt], [b_s])
                elif bias is None:
                    A(lambda e: e.activation(out=s_[:, 0:n], in_=pt[:, 0:n], func=func), [b_pt], [b_s])
                else:
                    A(lambda e: e.activation(out=s_[:, 0:n], in_=pt[:, 0:n], func=func, bias=bias), [b_pt] + list(bias_b), [b_s])
                src = s_[:, 0:n] if view is None else view(s_[:, 0:n])
                DM(dst_ap, src, r=[b_s])

            def store_bf2(self, dsts, pt, b_pt):
                s_, b_s = self.stgb.next()
                EVC(s_[:, 0:256], pt[:, 0:256], [b_pt], [b_s])
                for i_, d_ in enumerate(dsts):
                    DM(d_, s_[:, i_ * 128:(i_ + 1) * 128], r=[b_s])

            def store_both(self, dst32, dsts, pt, b_pt):
                if dst32 is None:
                    return self.store_bf2(dsts, pt, b_pt)
                s32, b_s32 = self.stg.next()
                EVC(s32[:, 0:256], pt[:, 0:256], [b_pt], [b_s32])
                DM(dst32, s32[:, 0:256], r=[b_s32])
                s_, b_s = self.stgb.next()
                G(lambda e: e.tensor_copy(out=s_[:, 0:256], in_=s32[:, 0:256]), [b_s32], [b_s])
                for i_, d_ in enumerate(dsts):
                    DM(d_, s_[:, i_ * 128:(i_ + 1) * 128], r=[b_s])

            def store_f32(self, dst_ap, pt, b_pt, n):
                s_, b_s = self.stg.next()
                EVC(s_[:, 0:n], pt[:, 0:n], [b_pt], [b_s])
                DM(dst_ap, s_[:, 0:n], r=[b_s])

        def phase_win(l):
            hT = cur["hT"]
            W = w_in[l]
            import os as _os
            with PhaseCtx("win"):
                pc = ProjCtx()
                stg, stgb = pc.stg, pc.stgb
                tmpr = Ring(fw, "tmpT", [128, 128], F32, 1)
                qkw = fw.sb("qkw", [128, 2], F32)
                gbT = fw.sb("gbT", [128, 48], F32)
                ropec = fw.sb("ropec", [128, 2048], F32)
                ropes = fw.sb("ropes", [128, 2048], F32)
                dtb_bc = fw.sb("dtb_bc", [128, 32], F32)
                b_rope = fw.buf(); b_dtb = fw.buf()
                DM(ropec[:], k_ropec, w=[b_rope])
                DM(ropes[:], k_ropes, w=[b_rope])
                all_dma_into(b_rope)
                b_qkw = load_T(qkw[:], 2, [(0, 1, qn_w[l:l + 1, :]), (1, 1, kn_w[l:l + 1, :])], tmpr)
                b_gbT = load_T(gbT[:], 48, [(0, 48, gate_b[l].rearrange("(c p) -> c p", p=128))], tmpr)
                DM(dtb_bc[:], dt_bias[l:l + 1, :].partition_broadcast(128).rearrange("p a n -> p (a n)"), w=[b_dtb])

                _wspecs = []
                if "qa" in _os.environ.get("DBG_WIN", "qa,va,qc,vc,z,xbc,dt,gl").split(","):
                    _wspecs += [(seg * 1024 + ct * 256, 256) for seg in (0, 1) for ct in range(4)]
                _wspecs += [(2048 + ct * 256, 256) for ct in range(4)]
                _wspecs += [(3072 + ct * 256, 256) for ct in range(4)] + [(4096, 256)]
                _wspecs += [(4352, 256)]
                _wspecs += [(4608 + ct * 256, 256) for ct in range(4)]
                _wspecs += [(5632 + ct * 256, 256) for ct in range(6)]
                _wspecs += [(7168, 32)]
                _wspecs += [(7200 + ct * 256, 256) for ct in range(24)]
                _wstream = pc.stream([(lambda c0=c0, nc_=nc_: pc.load_w(W[:, c0:c0 + nc_].rearrange("(k p) n -> p k n", p=128), 16, nc_))
                                      for (c0, nc_) in _wspecs])
                _wi = [0]

                def wtile(c0, ncols):
                    assert _wspecs[_wi[0]] == (c0, ncols), (_wspecs[_wi[0]], c0, ncols)
                    _wi[0] += 1
                    return _wstream.get()

                import os as _os
                _secs = _os.environ.get("DBG_WIN", "qa,va,qc,vc,z,xbc,dt,gl").split(",")
                for seg, dst in (((0, QAT), (1, KAT)) if "qa" in _secs else ()):
                    for ct in range(4):
                        wb, b_wb = wtile(seg * 1024 + ct * 256, 256)
                        for hh in range(2):
                            head = ct * 2 + hh
                            for (t0, tn) in TG:
                                pt, b_pt = psF.next()
                                mm_F(pt, b_pt, wb, b_wb, 16, hh * 128, hT, b_hT, t0, tn)
                                pc.store_bf(dst[head, :, t0:t0 + tn], pt, b_pt, tn)
                        if seg == 1:
                            for tile in range(4):
                                pt, b_pt = psF.next()
                                mm_T(pt, b_pt, wb, b_wb, 16, 256, hT, b_hT, tile)
                                pc.store_f32(o_nak[l, tile * 128:(tile + 1) * 128, ct * 256:(ct + 1) * 256], pt, b_pt, 256)
                for ct in (range(int(_os.environ.get("DBG_VA_CT", "4"))) if "va" in _secs else ()):
                    wb, b_wb = wtile(2048 + ct * 256, 256)
                    for tile in range(int(_os.environ.get("DBG_VA_T0", "0")), int(_os.environ.get("DBG_VA_T1", "20"))):
                        pt, b_pt = psF.next()
                        mm_T(pt, b_pt, wb, b_wb, 16, 256, hT, b_hT, tile)
                        pc.store_both(o_nav[l, tile * 128:(tile + 1) * 128, ct * 256:(ct + 1) * 256] if tile < 4 else None,
                                      [VAH[ct * 2 + hh_, tile * 128:(tile + 1) * 128, :] for hh_ in range(2)], pt, b_pt)
                for seg_c0, nheads, dst, wcol in (((3072, 8, QCT, 0), (4096, 2, KCT, 1)) if "qc" in _secs else ()):
                    for ct in range(nheads // 2):
                        wb, b_wb = wtile(seg_c0 + ct * 256, 256)
                        for hh in range(2):
                            head = ct * 2 + hh
                            for (t0, tn) in TG:
                                pt, b_pt = psF.next()
                                mm_F(pt, b_pt, wb, b_wb, 16, hh * 128, hT, b_hT, t0, tn)
                                sqb, b_sqb = stgb.next()
                                A(lambda e, sqb=sqb, pt=pt: e.activation(out=sqb[:], in_=pt[:], func=AF.Square), [b_pt], [b_sqb])
                                p2, b_p2 = psF.next()
                                P(lambda e, p2=p2, sqb=sqb: e.matmul(p2[:], lhsT=onesb[:], rhs=sqb[:], start=True, stop=True), [b_sqb] + CONST, [b_p2])
                                rs_, b_rs = stg.next()
                                A(lambda e, rs_=rs_, p2=p2: e.activation(out=rs_[:], in_=p2[:], func=AF.Ln, scale=1.0 / 128, bias=epsT[:]),
                                  [b_p2] + CONST, [b_rs])
                                A(lambda e, rs_=rs_: e.activation(out=rs_[:], in_=rs_[:], func=AF.Exp, scale=-0.5), [b_rs], [b_rs])
                                qn_, b_qn = stg.next()
                                V(lambda e, qn_=qn_, pt=pt, rs_=rs_, wcol=wcol: e.scalar_tensor_tensor(
                                    out=qn_[:], in0=pt[:], scalar=qkw[:, wcol:wcol + 1], in1=rs_[:], op0=ALU.mult, op1=ALU.mult),
                                  [b_pt, b_rs, b_qkw], [b_qn])
                                if t0 == 0:
                                    ob_, b_ob = stgb.next()
                                    G(lambda e, ob_=ob_, qn_=qn_: e.tensor_copy(out=ob_[:], in_=qn_[:]), [b_qn], [b_ob])
                                    DM(dst[head, :, t0:t0 + tn], ob_[:], r=[b_ob])
                                    if wcol == 1:
                                        p3, b_p3 = psF.next()
                                        for tt in range(4):
                                            P(lambda e, tt=tt, p3=p3, qn_=qn_: e.transpose(out=p3[:, tt * 128:(tt + 1) * 128], in_=qn_[:, tt * 128:(tt + 1) * 128],
                                                                                           identity=ident[:]), [b_qn] + CONST, [b_p3])
                                        s_, b_s = stg.next()
                                        V(lambda e, s_=s_, p3=p3: e.tensor_copy(out=s_[:], in_=p3[:]), [b_p3], [b_s])
                                        DM(o_gk[l, :, head * 128:(head + 1) * 128].rearrange("(tt p) d -> p tt d", p=128),
                                           s_[:].rearrange("p (tt d) -> p tt d", tt=4), r=[b_s])
                                else:
                                    ts0 = t0 - 512
                                    p3, b_p3 = psF.next()
                                    P(lambda e, p3=p3, qn_=qn_: e.matmul(p3[:], lhsT=ropeR[:], rhs=qn_[:], start=True, stop=True), [b_qn] + CONST, [b_p3])
                                    t2, b_t2 = stg.next()
                                    V(lambda e, t2=t2, p3=p3, ts0=ts0: e.tensor_tensor(out=t2[:], in0=p3[:], in1=ropes[:, ts0:ts0 + 512], op=ALU.mult),
                                      [b_p3, b_rope], [b_t2])
                                    G(lambda e, qn_=qn_, ts0=ts0: e.tensor_tensor(out=qn_[:], in0=qn_[:], in1=ropec[:, ts0:ts0 + 512], op=ALU.mult),
                                      [b_qn, b_rope], [b_qn])
                                    ob_, b_ob = stgb.next()
                                    V(lambda e, ob_=ob_, qn_=qn_, t2=t2: e.tensor_tensor(out=ob_[:], in0=qn_[:], in1=t2[:], op=ALU.add), [b_qn, b_t2], [b_ob])
                                    DM(dst[head, :, t0:t0 + tn], ob_[:], r=[b_ob])
                if "vc" in _secs:
                    wb, b_wb = wtile(4352, 256)
                for tile in (range(NT) if "vc" in _secs else ()):
                    pt, b_pt = psF.next()
                    mm_T(pt, b_pt, wb, b_wb, 16, 256, hT, b_hT, tile)
                    pc.store_both(o_gv[l, tile * 128:(tile + 1) * 128, :] if tile < 4 else None,
                                  [VCH[hh_, tile * 128:(tile + 1) * 128, :] for hh_ in range(2)], pt, b_pt)
                for ct in (range(4) if "z" in _secs else ()):
                    wb, b_wb = wtile(4608 + ct * 256, 256)
                    for tile in range(NT):
                        pt, b_pt = psF.next()
                        mm_T(pt, b_pt, wb, b_wb, 16, 256, hT, b_hT, tile)
                        pc.store_bf(SZ[tile * 128:(tile + 1) * 128, ct * 256:(ct + 1) * 256], pt, b_pt, 256, func=AF.Silu)
                for ct in (range(6) if "xbc" in _secs else ()):
                    wb, b_wb = wtile(5632 + ct * 256, 256)
                    for hh in range(2):
                        for (t0, tn) in TG:
                            pt, b_pt = psF.next()
                            mm_F(pt, b_pt, wb, b_wb, 16, hh * 128, hT, b_hT, t0, tn)
                            pc.store_f32(XBCT[ct * 2 + hh, :, t0:t0 + tn], pt, b_pt, tn)
                if "dt" in _secs:
                    wb, b_wb = wtile(7168, 32)
                for tile in (range(NT) if "dt" in _secs else ()):
                    pt, b_pt = psF.next()
                    mm_T(pt, b_pt, wb, b_wb, 16, 32, hT, b_hT, tile)
                    s_, b_s = stg.next()
                    V(lambda e, s_=s_, pt=pt: e.tensor_tensor(out=s_[:, 0:32], in0=pt[:, 0:32], in1=dtb_bc[:], op=ALU.add), [b_pt, b_dtb], [b_s])
                    A(lambda e, s_=s_: e.activation(out=s_[:, 32:64], in_=s_[:, 0:32], func=AF.Exp), [b_s], [b_s])
                    A(lambda e, s_=s_: e.activation(out=s_[:, 64:96], in_=s_[:, 32:64], func=AF.Ln, bias=1.0), [b_s], [b_s])
                    DM(DTS[tile * 128:(tile + 1) * 128, :], s_[:, 64:96], r=[b_s])
                for ct in (range(24) if "gl" in _secs else ()):
                    wb, b_wb = wtile(7200 + ct * 256, 256)
                    for hh in range(2):
                        cidx = ct * 2 + hh
                        for (t0, tn) in TG:
                            pt, b_pt = psF.next()
                            mm_F(pt, b_pt, wb, b_wb, 16, hh * 128, hT, b_hT, t0, tn)
                            pc.store_bf(GT[cidx, :, t0:t0 + tn], pt, b_pt, tn, func=AF.Sigmoid,
                                        bias=gbT[:, cidx:cidx + 1], bias_b=[b_gbT])
                scr_barrier(["QAT", "KAT", "VAH", "QCT", "KCT", "VCH", "SZ", "XBCT", "DTS", "GT"])

        class AttnCtx:
            def __init__(self):
                self.Sbuf = Ring(fw, "Sbuf", [128, 2560], F32, 2)
                self.Pbuf = Ring(fw, "Pbuf", [128, 2560], BF16, 2)
                self.PTb = Ring(fw, "PTb", [128, 20, 128], BF16, 2)
                self.Obuf = fw.sb("Obuf", [128, NT, 1024], BF16)
                self.b_O = [fw.buf() for _ in range(NT)]
                self.KTr = Ring(fw, "KTr", [128, T], BF16, 2)
                self.QTr = Ring(fw, "QTr", [128, T], BF16, 2)
                self.Vr = Ring(fw, "Vr", [128, NT, 128], BF16, 2)
                self.CKT = Ring(fw, "CKT", [128, 512], BF16, 2)
                self.CVb = Ring(fw, "CVb", [128, 4, 128], BF16, 2)
                self.cst = Ring(fw, "cst", [128, 4, 128], F32, 2)
                self.ost = Ring(fw, "ost", [128, 1024], BF16, 2)

            def stageA1(self, qT_ap, b_q, kparts):
                parts = []
                for (k_ap, kb, bias_ap, bb) in kparts:
                    n = k_ap.shape[1]
                    pt, b_pt = psF.next()
                    P(lambda e, k_ap=k_ap, pt=pt, n=n: e.matmul(pt[:, 0:n], lhsT=qT_ap, rhs=k_ap, start=True, stop=True),
                      [b_q] + list(kb), [b_pt])
                    parts.append((pt, b_pt, n, bias_ap, bb))
                return parts

            def stageA2(self, parts, scale):
                S_, b_S = self.Sbuf.next()
                nk = 0
                for (pt, b_pt, n, bias_ap, bb) in parts:
                    if bias_ap is None:
                        evac_rr[0] ^= 1
                        if evac_rr[0]:
                            A(lambda e, pt=pt, n=n, nk=nk: e.activation(out=S_[:, nk:nk + n], in_=pt[:, 0:n], func=AF.Copy, scale=scale), [b_pt], [b_S])
                        else:
                            V(lambda e, pt=pt, n=n, nk=nk: e.tensor_scalar(out=S_[:, nk:nk + n], in0=pt[:, 0:n], scalar1=scale, scalar2=None, op0=ALU.mult),
                              [b_pt], [b_S])
                    else:
                        V(lambda e, pt=pt, n=n, nk=nk, bias_ap=bias_ap: e.scalar_tensor_tensor(
                            out=S_[:, nk:nk + n], in0=pt[:, 0:n], scalar=scale, in1=bias_ap, op0=ALU.mult, op1=ALU.add),
                          [b_pt] + list(bb), [b_S])
                    nk += n
                s_, b_s = stat.next()
                V(lambda e: e.tensor_reduce(out=s_[:, 0:1], in_=S_[:, 0:nk], axis=AX.X, op=ALU.max, negate=True), [b_S], [b_s])
                P_, b_P = self.Pbuf.next()
                A(lambda e: e.activation(out=P_[:, 0:nk], in_=S_[:, 0:nk], func=AF.Exp, bias=s_[:, 0:1], accum_out=s_[:, 1:2]),
                  [b_S, b_s], [b_P, b_s])
                V(lambda e: e.reciprocal(out=s_[:, 2:3], in_=s_[:, 1:2]), [b_s], [b_s])
                return (P_, b_P, s_, b_s, nk)

            def stageB1(self, st_):
                P_, b_P, s_, b_s, nk = st_
                nb = nk // 128
                PT_, b_PT = self.PTb.next()
                for b0 in range(0, nb, 8):
                    bn = min(8, nb - b0)
                    pb, b_pb = psB.next()
                    for i in range(bn):
                        P(lambda e, i=i, pb=pb, b0=b0: e.transpose(out=pb[:, i * 128:(i + 1) * 128],
                                                                   in_=P_[:, (b0 + i) * 128:(b0 + i + 1) * 128], identity=identb[:]),
                          [b_P] + CONST, [b_pb])
                    EVC(PT_[:, b0:b0 + bn, :], pb[:, 0:bn * 128].rearrange("p (k t) -> p k t", k=bn), [b_pb], [b_PT])
                return (PT_, b_PT, nb)

            def stageB2(self, b1, st_, vblocks, o_ap, b_o):
                PT_, b_PT, nb = b1
                P_, b_P, s_, b_s, nk = st_
                po, b_po = psF.next()
                for i, (v_ap, vb) in enumerate(vblocks):
                    P(lambda e, i=i, v_ap=v_ap: e.matmul(po[:, 0:128], lhsT=PT_[:, i, :], rhs=v_ap, start=(i == 0), stop=(i == nb - 1)),
                      [b_PT] + list(vb), [b_po])
                V(lambda e: e.tensor_scalar(out=o_ap, in0=po[:, 0:128], scalar1=s_[:, 2:3], scalar2=None, op0=ALU.mult),
                  [b_po, b_s], [b_o])

            def run(self, jobs, scale, hook=None):
                prev = None
                for job in jobs:
                    qT_ap, b_q, kparts, vblocks, o_ap, b_o = job
                    a1 = self.stageA1(qT_ap, b_q, kparts)
                    b1 = self.stageB1(prev[0]) if prev is not None else None
                    st_ = self.stageA2(a1, scale)
                    if prev is not None:
                        self.stageB2(b1, *prev)
                        if hook is not None:
                            hook()
                    prev = (st_, vblocks, o_ap, b_o)
                if prev is not None:
                    b1 = self.stageB1(prev[0])
                    self.stageB2(b1, *prev)
                    if hook is not None:
                        hook()

            def o_store(self, dst, nm):
                for tile in range(NT):
                    pb, b_pb = psB.next()
                    for hh in range(8):
                        P(lambda e, hh=hh, pb=pb, tile=tile: e.transpose(out=pb[:, hh * 128:(hh + 1) * 128], in_=self.Obuf[:, tile, hh * 128:(hh + 1) * 128],
                                                                         identity=identb[:]), [self.b_O[tile]] + CONST, [b_pb])
                    s_, b_s = self.ost.next()
                    EVC(s_[:], pb[:], [b_pb], [b_s])
                    DM(dst[:, :, tile * 128:(tile + 1) * 128].rearrange("c p t -> p c t"),
                       s_[:].rearrange("p (c t) -> p c t", c=8), r=[b_s])
                scr_barrier([nm])

            def load_ctx(self, cache_k_ap, cache_v_ap):
                c1, b_c1 = self.cst.next()
                DM(c1[:], cache_k_ap.rearrange("(tt p) d -> p tt d", p=128), w=[b_c1])
                p3, b_p3 = psF.next()
                for tt in range(4):
                    P(lambda e, tt=tt: e.transpose(out=p3[:, tt * 128:(tt + 1) * 128], in_=c1[:, tt, :], identity=ident[:]),
                      [b_c1] + CONST, [b_p3])
                ck, b_ck = self.CKT.next()
                V(lambda e: e.tensor_copy(out=ck[:], in_=p3[:]), [b_p3], [b_ck])
                c2, b_c2_ = self.cst.next()
                DM(c2[:], cache_v_ap.rearrange("(tt p) d -> p tt d", p=128), w=[b_c2_])
                cvb, b_cv = self.CVb.next()
                G(lambda e: e.tensor_copy(out=cvb[:], in_=c2[:]), [b_c2_], [b_cv])
                return ck, b_ck, cvb, b_cv

        def phase_na(l):
            scale = 128 ** -0.5
            with PhaseCtx("na"):
                ac = AttnCtx()
                biasm = fw.sb("biasm", [128, 5, 640], F32); b_biasm = fw.buf()
                namask = fw.sb("namask", [128, 5, 640], F32); b_namask = fw.buf()
                G2p = fw.sb("G2p", [128, 18 * 64], F32); b_G2p = fw.buf()
                G2 = fw.sb("G2", [128, 18 * 64], F32); b_G2 = fw.buf()
                zeroT = fw.sb("zeroT", [1, 2048], F32); b_z = fw.buf()
                V(lambda e: e.memset(zeroT[:], 0.0), [], [b_z])
                DM(namask[:], k_namask.rearrange("a p n -> p a n"), w=[b_namask])
                bR = b_scr["RPBP"]
                DM(RPBP[0:2048].rearrange("(p n) -> p n", p=1), zeroT[:], r=[b_z, bR], w=[bR])
                DM(RPBP[2048 + 3720:2048 + 3720 + 2048].rearrange("(p n) -> p n", p=1), zeroT[:], r=[b_z])
                DM(RPBP[2048:2048 + 3720].rearrange("(p n) -> p n", p=1), na_rpb[l:l + 1, :])
                scr_barrier(["RPBP"])
                for head in range(8):
                    kt, b_kt = ac.KTr.next()
                    DM(kt[:], KAT[head], r=[b_scr["KAT"]], w=[b_kt])
                    qt, b_qt = ac.QTr.next()
                    DM(qt[:], QAT[head], r=[b_scr["QAT"]], w=[b_qt])
                    vt, b_vt = ac.Vr.next()
                    DM(vt[:], VAH[head].rearrange("(n p) d -> p n d", p=128), r=[b_scr["VAH"]], w=[b_vt])
                    ck, b_ck, cvb, b_cv = ac.load_ctx(c_nak[l, :, head, :], c_nav[l, :, head, :])
                    evs = []
                    for half in range(2):
                        base = 2048 + head * 465 + (-1 - half) * 31 - 48
                        src = AP(RPBP.tensor, base, [[1, 64], [31, 18], [1, 64]])
                        DM(G2p[half * 64:(half + 1) * 64, :].rearrange("p (a c) -> p a c", a=18), src, r=[bR], w=[b_G2p] if half == 0 else [])
                    all_dma_into(b_G2p)
                    for c0 in range(0, 1152, 512):
                        n = min(512, 1152 - c0)
                        pt, b_pt = psF.next()
                        P(lambda e, pt=pt, c0=c0, n=n: e.matmul(pt[:, 0:n], lhsT=flipJ[:], rhs=G2p[:, c0:c0 + n], start=True, stop=True),
                          [b_G2p] + CONST, [b_pt])
                        V(lambda e, pt=pt, c0=c0, n=n: e.tensor_copy(out=G2[:, c0:c0 + n], in_=pt[:, 0:n]), [b_pt], [b_G2])
                    for ti, jt in enumerate([0, 1, 5, 14, 15]):
                        tw = min(max(jt - 2, 0), 11)
                        a0 = 2 * (tw - jt) + 7 + 1
                        V(lambda e, ti=ti, a0=a0: e.tensor_tensor(out=biasm[:, ti, :], in0=G2[:, a0 * 64:(a0 + 10) * 64],
                                                                  in1=namask[:, ti, :], op=ALU.add),
                          [b_G2, b_namask], [b_biasm])
                    jobs = []
                    for (s0, sl, r) in SEQS[:2]:
                        for tile in range(s0 // 128, (s0 + sl) // 128):
                            jobs.append((qt[:, tile * 128:(tile + 1) * 128], b_qt,
                                         [(kt[:, s0:s0 + sl], [b_kt], None, [])],
                                         [(vt[:, s0 // 128 + i, :], [b_vt]) for i in range(sl // 128)],
                                         ac.Obuf[:, tile, head * 128:(head + 1) * 128], ac.b_O[tile]))
                    for jt in range(16):
                        tile = 4 + jt
                        tw = min(max(jt - 2, 0), 11)
                        ti = na_tile_type(jt)
                        k0 = 512 + tw * 128
                        kparts = [(kt[:, k0:k0 + 512], [b_kt], biasm[:, ti, 0:512], [b_biasm]),
                                  (kt[:, k0 + 512:k0 + 640], [b_kt], biasm[:, ti, 512:640], [b_biasm]),
                                  (ck[:], [b_ck], None, [])]
                        vbl = [(vt[:, 4 + tw + i, :], [b_vt]) for i in range(5)] + [(cvb[:, i, :], [b_cv]) for i in range(4)]
                        jobs.append((qt[:, tile * 128:(tile + 1) * 128], b_qt, kparts, vbl,
                                     ac.Obuf[:, tile, head * 128:(head + 1) * 128], ac.b_O[tile]))
                    ac.run(jobs, scale)
                ac.o_store(OXT[0], "OXT0")

        def phase_gqa(l):
            scale = 128 ** -0.5
            with PhaseCtx("gqa"):
                ac = AttnCtx()
                mgen = mods_steps(l + 1) if l + 1 < depth else iter(())
                mcnt = [0]

                def mstep():
                    mcnt[0] += 1
                    if mcnt[0] % 3 == 0:
                        next(mgen, None)
                for kv in range(2):
                    kt, b_kt = ac.KTr.next()
                    DM(kt[:], KCT[kv], r=[b_scr["KCT"]], w=[b_kt])
                    vt, b_vt = ac.Vr.next()
                    DM(vt[:], VCH[kv].rearrange("(n p) d -> p n d", p=128), r=[b_scr["VCH"]], w=[b_vt])
                    ck, b_ck, cvb, b_cv = ac.load_ctx(c_gk[l, :, kv, :], c_gv[l, :, kv, :])
                    for g in range(4):
                        head = kv * 4 + g
                        qt, b_qt = ac.QTr.next()
                        DM(qt[:], QCT[head], r=[b_scr["QCT"]], w=[b_qt])
                        jobs = []
                        for (s0, sl, r) in SEQS[:2]:
                            for tile in range(s0 // 128, (s0 + sl) // 128):
                                jobs.append((qt[:, tile * 128:(tile + 1) * 128], b_qt,
                                             [(kt[:, s0:s0 + sl], [b_kt], None, [])],
                                             [(vt[:, s0 // 128 + i, :], [b_vt]) for i in range(sl // 128)],
                                             ac.Obuf[:, tile, head * 128:(head + 1) * 128], ac.b_O[tile]))
                        for jt in range(16):
                            tile = 4 + jt
                            kparts = [(kt[:, 512 + i * 512:512 + (i + 1) * 512], [b_kt], None, []) for i in range(4)]
                            kparts.append((ck[:], [b_ck], None, []))
                            vbl = [(vt[:, 4 + i, :], [b_vt]) for i in range(16)] + [(cvb[:, i, :], [b_cv]) for i in range(4)]
                            jobs.append((qt[:, tile * 128:(tile + 1) * 128], b_qt, kparts, vbl,
                                         ac.Obuf[:, tile, head * 128:(head + 1) * 128], ac.b_O[tile]))
                        ac.run(jobs, scale, hook=mstep)
                for _ in mgen:
                    pass
                ac.o_store(OXT[2], "OXT2")

        def phase_ssd(l):
            with PhaseCtx("ssd"):
                xtok = fw.sb("xtok", [128, 16, 1024], BF16); b_xtok = [fw.buf() for _ in range(16)]
                yacc = fw.sb("yacc", [128, 16, 1024], F32); b_yacc = [fw.buf() for _ in range(16)]
                Btok = fw.sb("Btok", [128, 16, 256], BF16); b_Btok = [fw.buf() for _ in range(16)]
                BTs = fw.sb("BTs", [128, 2, 2048], BF16); b_BTs = [fw.buf() for _ in range(4)]
                CTs = fw.sb("CTs", [128, 2, 2048], BF16); b_CTs = [fw.buf() for _ in range(4)]
                dts = fw.sb("dts", [128, 16, 32], F32); b_dts = fw.buf()
                cwT = fw.sb("cwT", [128, 48], F32)
                Aneg = fw.sb("Aneg", [128, 32], F32); b_Aneg = fw.buf()
                dtot = fw.sb("dtot", [128, 16], F32); b_dtot = fw.buf()
                dsk = fw.sb("dsk", [128, 32], F32)
                hst = fw.sb("hst", [128, 1024], F32); b_hst = fw.buf()
                hstb = fw.sb("hstb", [128, 1024], BF16); b_hstb = fw.buf()
                ssdnw = fw.sb("ssdnw", [128, 1024], F32); b_ssdnw = fw.buf()
                st_in = fw.sb("st_in", [128, 8, 128], F32); b_stin = fw.buf()
                xcin = Ring(fw, "xcin", [128, 514], F32, 2)
                xc1 = Ring(fw, "xc1", [128, 512], F32, 2)
                xcb = Ring(fw, "xcb", [128, 512], BF16, 2)
                small = Ring(fw, "small", [128, 64], F32, 4)
                lhs_r = Ring(fw, "lhs_r", [128, 128], F32, 4)
                Lr = Ring(fw, "Lr", [128, 512], F32, 5)
                cbm = Ring(fw, "cbm", [128, 2, 128], F32, 2)
                MTr = Ring(fw, "MTr", [128, 128], BF16, 4)
                xwr = Ring(fw, "xwr", [128, 1024], BF16, 2)
                tmpy = Ring(fw, "tmpy", [128, 512], F32, 2)
                szr = Ring(fw, "szr", [128, 1024], BF16, 1)
                obr = Ring(fw, "obr", [128, 1024], BF16, 1)
                t1r = Ring(fw, "t1r", [128, 1024], F32, 1)
                ost = Ring(fw, "ost", [128, 1024], BF16, 2)
                stg = Ring(fw, "stg", [128, 512], F32, 2)
                tmpr = Ring(fw, "tmpT", [128, 128], F32, 1)
                b_cw = load_T(cwT[:], 48, [(0, 36, conv_w[l].rearrange("j (c p) -> (j c) p", p=128)),
                                           (36, 12, conv_b[l].rearrange("(c p) -> c p", p=128))], tmpr)
                DM(Aneg[:], a_log[l:l + 1, :].partition_broadcast(128).rearrange("p a n -> p (a n)"), w=[b_Aneg])
                A(lambda e: e.activation(out=Aneg[:], in_=Aneg[:], func=AF.Exp), [b_Aneg], [b_Aneg])
                V(lambda e: e.tensor_scalar(out=Aneg[:], in0=Aneg[:], scalar1=-1.0, scalar2=None, op0=ALU.mult), [b_Aneg], [b_Aneg])
                DM(dsk[:], ssd_d[l:l + 1, :].partition_broadcast(128).rearrange("p a n -> p (a n)"), w=[b_dtot])
                V(lambda e: e.tensor_tensor(out=dtot[:], in0=dsk[:, 0:16], in1=dsk[:, 16:32], op=ALU.add), [b_dtot], [b_dtot])
                DM(ssdnw[:], ssd_nw[l:l + 1, :].partition_broadcast(128).rearrange("p a n -> p (a n)"), w=[b_ssdnw])
                for si, (s0, sl, r) in enumerate(SEQS):
                    nch = sl // 128
                    for b0 in range(0, sl, 512):
                        bl = min(512, sl - b0)
                        gi = b0 // 512
                        ch0 = b0 // 128
                        nt_ = bl // 128
                        for c in range(12):
                            xi, b_xi = xcin.next()
                            lo = 1 if b0 == 0 else 0
                            hi = 1 if b0 + bl == sl else 0
                            if lo:
                                V(lambda e, xi=xi: e.memset(xi[:, 0:1], 0.0), [], [b_xi])
                            if hi:
                                V(lambda e, xi=xi, bl=bl: e.memset(xi[:, bl + 1:bl + 2], 0.0), [], [b_xi])
                            DM(xi[:, lo:bl + 2 - hi], XBCT[c, :, s0 + b0 - 1 + lo:s0 + b0 + bl + 1 - hi], r=[b_scr["XBCT"]], w=[b_xi])
                            x1, b_x1 = xc1.next()
                            V(lambda e, xi=xi, x1=x1, c=c, bl=bl: e.tensor_scalar(out=x1[:, 0:bl], in0=xi[:, 0:bl], scalar1=cwT[:, c:c + 1],
                                                                                  scalar2=cwT[:, 36 + c:37 + c], op0=ALU.mult, op1=ALU.add),
                              [b_xi, b_cw], [b_x1])
                            V(lambda e, xi=xi, x1=x1, c=c, bl=bl: e.scalar_tensor_tensor(out=x1[:, 0:bl], in0=xi[:, 1:bl + 1], scalar=cwT[:, 12 + c:13 + c],
                                                                                         in1=x1[:, 0:bl], op0=ALU.mult, op1=ALU.add),
                              [b_xi, b_cw, b_x1], [b_x1])
                            V(lambda e, xi=xi, x1=x1, c=c, bl=bl: e.scalar_tensor_tensor(out=x1[:, 0:bl], in0=xi[:, 2:bl + 2], scalar=cwT[:, 24 + c:25 + c],
                                                                                         in1=x1[:, 0:bl], op0=ALU.mult, op1=ALU.add),
                              [b_xi, b_cw, b_x1], [b_x1])
                            if c < 8:
                                xb_, b_xb = xcb.next()
                                A(lambda e, x1=x1, xb_=xb_, bl=bl: e.activation(out=xb_[:, 0:bl], in_=x1[:, 0:bl], func=AF.Silu), [b_x1], [b_xb])
                                pb, b_pb = psB.next()
                                for tt in range(nt_):
                                    P(lambda e, tt=tt, pb=pb, xb_=xb_: e.transpose(out=pb[:, tt * 128:(tt + 1) * 128],
                                                                                   in_=xb_[:, tt * 128:(tt + 1) * 128], identity=identb[:]),
                                      [b_xb] + CONST, [b_pb])
                                V(lambda e, pb=pb, ch0=ch0, nt_=nt_, c=c: e.tensor_copy(
                                    out=xtok[:, ch0:ch0 + nt_, c * 128:(c + 1) * 128],
                                    in_=pb[:, 0:nt_ * 128].rearrange("p (t d) -> p t d", t=nt_)), [b_pb], [b_xtok[ch0 + tt] for tt in range(nt_)])
                            elif c < 10:
                                g_ = c - 8
                                A(lambda e, x1=x1, g_=g_, b0=b0, bl=bl: e.activation(out=BTs[:, g_, b0:b0 + bl], in_=x1[:, 0:bl], func=AF.Silu),
                                  [b_x1], [b_BTs[gi]])
                                pb, b_pb = psB.next()
                                for tt in range(nt_):
                                    P(lambda e, tt=tt, pb=pb, g_=g_, b0=b0: e.transpose(out=pb[:, tt * 128:(tt + 1) * 128],
                                                                                        in_=BTs[:, g_, b0 + tt * 128:b0 + (tt + 1) * 128], identity=identb[:]),
                                      [b_BTs[gi]] + CONST, [b_pb])
                                V(lambda e, pb=pb, ch0=ch0, nt_=nt_, g_=g_: e.tensor_copy(
                                    out=Btok[:, ch0:ch0 + nt_, g_ * 128:(g_ + 1) * 128],
                                    in_=pb[:, 0:nt_ * 128].rearrange("p (t d) -> p t d", t=nt_)), [b_pb], [b_Btok[ch0 + tt] for tt in range(nt_)])
                            else:
                                g_ = c - 10
                                A(lambda e, x1=x1, g_=g_, b0=b0, bl=bl: e.activation(out=CTs[:, g_, b0:b0 + bl], in_=x1[:, 0:bl], func=AF.Silu),
                                  [b_x1], [b_CTs[gi]])
                    DM(dts[:, 0:nch, :], DTS[s0:s0 + sl, :].rearrange("(n p) d -> p n d", p=128), r=[b_scr["DTS"]], w=[b_dts])
                    def run_dir(di, si=si, s0=s0, sl=sl, r=r, nch=nch):
                        Um = ssdm[:, di * 3 + 0, :]
                        TRI = ssdm[:, di * 3 + 1, :]
                        MSK = ssdm[:, di * 3 + 2, :]
                        ce = 127 if di == 0 else 0
                        if r == 0:
                            V(lambda e: e.memset(hst[:], 0.0), [], [b_hst])
                        else:
                            DM(st_in[:], c_ssd[l, di].rearrange("(c q) n -> q c n", q=128), w=[b_stin])
                            for half in range(2):
                                pt, b_pt = psF.next()
                                for cc in range(4):
                                    c8 = half * 4 + cc
                                    P(lambda e, cc=cc, c8=c8, pt=pt: e.transpose(out=pt[:, cc * 128:(cc + 1) * 128], in_=st_in[:, c8, :], identity=ident[:]),
                                      [b_stin] + CONST, [b_pt])
                                V(lambda e, pt=pt, half=half: e.tensor_copy(out=hst[:, half * 512:(half + 1) * 512], in_=pt[:]), [b_pt], [b_hst])
                        chunks = range(nch) if di == 0 else range(nch - 1, -1, -1)
                        for c in chunks:
                            gi = c // 4
                            sm, b_sm = small.next()
                            V(lambda e, sm=sm, c=c: e.tensor_tensor(out=sm[:, 0:16], in0=dts[:, c, di * 16:(di + 1) * 16], in1=Aneg[:, di * 16:(di + 1) * 16],
                                                                    op=ALU.mult), [b_dts, b_Aneg], [b_sm])
                            pt, b_pt = psF.next()
                            P(lambda e, pt=pt, sm=sm: e.matmul(pt[:, 0:16], lhsT=TRI, rhs=sm[:, 0:16], start=True, stop=True), [b_sm] + CONST, [b_pt])
                            P(lambda e, pt=pt, sm=sm: e.matmul(pt[:, 16:32], lhsT=ones[:], rhs=sm[:, 0:16], start=True, stop=True), [b_sm] + CONST, [b_pt])
                            A(lambda e, pt=pt, sm=sm: e.activation(out=sm[:, 16:48], in_=pt[:, 0:32], func=AF.Exp), [b_pt], [b_sm])
                            cb_, b_cb = cbm.next()
                            pt2, b_pt2 = psF.next()
                            for g_ in range(2):
                                P(lambda e, g_=g_, pt2=pt2, c=c: e.matmul(pt2[:, g_ * 128:(g_ + 1) * 128], lhsT=BTs[:, g_, c * 128:(c + 1) * 128],
                                                                          rhs=CTs[:, g_, c * 128:(c + 1) * 128], start=True, stop=True),
                                  [b_BTs[gi], b_CTs[gi]], [b_pt2])
                            V(lambda e, cb_=cb_, pt2=pt2: e.tensor_tensor(out=cb_[:], in0=pt2[:, 0:256].rearrange("p (g l) -> p g l", g=2),
                                                                          in1=MSK.unsqueeze(1).to_broadcast([128, 2, 128]), op=ALU.mult),
                              [b_pt2] + CONST, [b_cb])
                            G(lambda e: e.tensor_copy(out=hstb[:], in_=hst[:]), [b_hst], [b_hstb])
                            pyo = []
                            for g_ in range(2):
                                po, b_po = psF.next()
                                P(lambda e, g_=g_, po=po, c=c: e.matmul(po[:], lhsT=CTs[:, g_, c * 128:(c + 1) * 128], rhs=hstb[:, g_ * 512:(g_ + 1) * 512],
                                                                        start=True, stop=True), [b_CTs[gi], b_hstb], [b_po])
                                ty, b_ty = tmpy.next()
                                V(lambda e, g_=g_, po=po, ty=ty, sm=sm: e.tensor_tensor(
                                    out=ty[:].rearrange("p (h q) -> p h q", h=8), in0=po[:].rearrange("p (h q) -> p h q", h=8),
                                    in1=sm[:, 16 + g_ * 8:16 + (g_ + 1) * 8].unsqueeze(2).to_broadcast([128, 8, 64]), op=ALU.mult),
                                  [b_po, b_sm], [b_ty])
                                pyo.append((ty, b_ty))
                            Ls = []
                            for q4 in range(4):
                                pl, b_pl = psF.next()
                                for hh in range(4):
                                    h_ = q4 * 4 + hh
                                    lh, b_lh = lhs_r.next()
                                    V(lambda e, lh=lh, h_=h_, sm=sm: e.tensor_scalar(out=lh[:], in0=Um, scalar1=sm[:, h_:h_ + 1], scalar2=None, op0=ALU.mult),
                                      [b_sm] + CONST, [b_lh])
                                    P(lambda e, lh=lh, pl=pl, hh=hh: e.matmul(pl[:, hh * 128:(hh + 1) * 128], lhsT=lh[:], rhs=TRI, start=True, stop=True),
                                      [b_lh] + CONST, [b_pl])
                                Lq, b_L = Lr.next()
                                A(lambda e, Lq=Lq, pl=pl: e.activation(out=Lq[:], in_=pl[:], func=AF.Exp), [b_pl], [b_L])
                                Ls.append((Lq, b_L))
                            sw, b_sw = small.next()
                            for q4 in range(4):
                                Lq, b_L = Ls[q4]
                                V(lambda e, Lq=Lq, sw=sw, q4=q4, c=c: e.tensor_tensor(
                                    out=sw[:, q4 * 4:(q4 + 1) * 4], in0=Lq[:].rearrange("p (h l) -> p h l", h=4)[:, :, ce],
                                    in1=dts[:, c, di * 16 + q4 * 4:di * 16 + (q4 + 1) * 4], op=ALU.mult),
                                  [b_L, b_dts], [b_sw])
                            for g_ in range(2):
                                pd, b_pd = psF.next()
                                for hh in range(8):
                                    h_ = g_ * 8 + hh
                                    Lq, b_L = Ls[h_ // 4]
                                    mt, b_mt = MTr.next()
                                    V(lambda e, mt=mt, Lq=Lq, h_=h_, g_=g_, cb_=cb_, c=c: e.scalar_tensor_tensor(
                                        out=mt[:], in0=Lq[:, (h_ % 4) * 128:(h_ % 4 + 1) * 128], scalar=dts[:, c, di * 16 + h_:di * 16 + h_ + 1],
                                        in1=cb_[:, g_, :], op0=ALU.mult, op1=ALU.mult), [b_L, b_dts, b_cb], [b_mt])
                                    P(lambda e, mt=mt, pd=pd, hh=hh, h_=h_, c=c: e.matmul(pd[:, hh * 64:(hh + 1) * 64], lhsT=mt[:],
                                                                                         rhs=xtok[:, c, h_ * 64:(h_ + 1) * 64], start=True, stop=True),
                                      [b_mt, b_xtok[c]], [b_pd])
                                ty, b_ty = pyo[g_]
                                if di == 0:
                                    V(lambda e, g_=g_, pd=pd, ty=ty, c=c: e.tensor_tensor(out=yacc[:, c, g_ * 512:(g_ + 1) * 512], in0=pd[:], in1=ty[:], op=ALU.add),
                                      [b_pd, b_ty], [b_yacc[c]])
                                else:
                                    V(lambda e, pd=pd, ty=ty: e.tensor_tensor(out=ty[:], in0=pd[:], in1=ty[:], op=ALU.add), [b_pd, b_ty], [b_ty])
                                    G(lambda e, g_=g_, ty=ty, c=c: e.tensor_tensor(out=yacc[:, c, g_ * 512:(g_ + 1) * 512], in0=yacc[:, c, g_ * 512:(g_ + 1) * 512],
                                                                                  in1=ty[:], op=ALU.add), [b_ty, b_yacc[c]], [b_yacc[c]])
                            xw, b_xw = xwr.next()
                            V(lambda e, xw=xw, sw=sw, c=c: e.tensor_tensor(out=xw[:].rearrange("p (h q) -> p h q", h=16),
                                                                           in0=xtok[:, c, :].rearrange("p (h q) -> p h q", h=16),
                                                                           in1=sw[:, 0:16].unsqueeze(2).to_broadcast([128, 16, 64]), op=ALU.mult),
                              [b_xtok[c], b_sw], [b_xw])
                            V(lambda e, sm=sm: e.tensor_tensor(out=hst[:].rearrange("p (h q) -> p h q", h=16), in0=hst[:].rearrange("p (h q) -> p h q", h=16),
                                                               in1=sm[:, 32:48].unsqueeze(2).to_broadcast([128, 16, 64]), op=ALU.mult),
                              [b_hst, b_sm], [b_hst])
                            for g_ in range(2):
                                pst, b_pst = psF.next()
                                P(lambda e, g_=g_, pst=pst, xw=xw, c=c: e.matmul(pst[:], lhsT=Btok[:, c, g_ * 128:(g_ + 1) * 128], rhs=xw[:, g_ * 512:(g_ + 1) * 512],
                                                                                start=True, stop=True), [b_Btok[c], b_xw], [b_pst])
                                V(lambda e, g_=g_, pst=pst: e.tensor_tensor(out=hst[:, g_ * 512:(g_ + 1) * 512], in0=hst[:, g_ * 512:(g_ + 1) * 512], in1=pst[:], op=ALU.add),
                                  [b_pst, b_hst], [b_hst])
                        if r == 0:
                            for half in range(2):
                                pt, b_pt = psF.next()
                                for cc in range(4):
                                    c8 = half * 4 + cc
                                    P(lambda e, cc=cc, c8=c8, pt=pt: e.transpose(out=pt[:, cc * 128:(cc + 1) * 128], in_=hst[:, c8 * 128:(c8 + 1) * 128], identity=ident[:]),
                                      [b_hst] + CONST, [b_pt])
                                s_, b_s = stg.next()
                                V(lambda e, pt=pt, s_=s_: e.tensor_copy(out=s_[:], in_=pt[:]), [b_pt], [b_s])
                                DM(o_ssd[si, l, di, half * 512:(half + 1) * 512, :].rearrange("(c q) n -> q c n", q=128),
                                   s_[:].rearrange("p (c n) -> p c n", c=4), r=[b_s])
                    run_dir(0)
                    run_dir(1)
                    for c in range(nch):
                        tile = s0 // 128 + c
                        sz_, b_sz = szr.next()
                        DM(sz_[:], SZ[tile * 128:(tile + 1) * 128, :], r=[b_scr["SZ"]], w=[b_sz])
                        t1, b_t1 = t1r.next()
                        V(lambda e, t1=t1, c=c: e.tensor_tensor(out=t1[:].rearrange("p (h q) -> p h q", h=16),
                                                                in0=xtok[:, c, :].rearrange("p (h q) -> p h q", h=16),
                                                                in1=dtot[:].unsqueeze(2).to_broadcast([128, 16, 64]), op=ALU.mult),
                          [b_xtok[c], b_dtot], [b_t1])
                        G(lambda e, t1=t1, c=c: e.tensor_tensor(out=t1[:], in0=t1[:], in1=yacc[:, c, :], op=ALU.add), [b_t1, b_yacc[c]], [b_t1])
                        V(lambda e, t1=t1, sz_=sz_: e.tensor_tensor(out=t1[:], in0=t1[:], in1=sz_[:], op=ALU.mult), [b_t1, b_sz], [b_t1])
                        r_ap, b_r = rstd_of(t1[:], b_t1, 1024)
                        ob_, b_ob = obr.next()
                        V(lambda e, t1=t1, r_ap=r_ap, ob_=ob_: e.scalar_tensor_tensor(out=ob_[:], in0=t1[:], scalar=r_ap, in1=ssdnw[:],
                                                                                      op0=ALU.mult, op1=ALU.mult), [b_t1, b_r, b_ssdnw], [b_ob])
                        pb, b_pb = psB.next()
                        for hh in range(8):
                            P(lambda e, hh=hh, pb=pb, ob_=ob_: e.transpose(out=pb[:, hh * 128:(hh + 1) * 128], in_=ob_[:, hh * 128:(hh + 1) * 128], identity=identb[:]),
                              [b_ob] + CONST, [b_pb])
                        s_, b_s = ost.next()
                        EVC(s_[:], pb[:], [b_pb], [b_s])
                        DM(OXT[1][:, :, tile * 128:(tile + 1) * 128].rearrange("c p t -> p c t"),
                           s_[:].rearrange("p (c t) -> p c t", c=8), r=[b_s])
                scr_barrier(["OXT1"])

        def phase_branch(l):
            hT = cur["hT"]
            with PhaseCtx("branch"):
                pc = ProjCtx(kcmax=8)
                stg, stgb = pc.stg, pc.stgb
                ox2 = fw.sb("ox2", [128, 8, T], BF16); b_ox = fw.buf()
                DM(hT[:, 0:8, :], OXT[0].rearrange("c p t -> p c t"), r=[b_scr["OXT0"]], w=[b_hT])
                DM(hT[:, 8:16, :], OXT[1].rearrange("c p t -> p c t"), r=[b_scr["OXT1"]])
                all_dma_into(b_hT)
                DM(ox2[:], OXT[2].rearrange("c p t -> p c t"), r=[b_scr["OXT2"]], w=[b_ox])
                srcs = [(hT, b_hT, 0), (hT, b_hT, 8), (ox2, b_ox, 0)]
                wbr = [Ring(fw, "wbr%d" % i, [128, 8, 256], BF16, 2) for i in range(3)]
                def _ld3(ct):
                    wts_ = []
                    for br in range(3):
                        ws, b_ws = pc.wst.next()
                        DM(ws[:, 0:8, :], w_br[l, br, :, ct * 256:(ct + 1) * 256].rearrange("(k p) n -> p k n", p=128), w=[b_ws])
                        wb, b_wb = wbr[br].next()
                        G(lambda e, wb=wb, ws=ws: e.tensor_copy(out=wb[:], in_=ws[:, 0:8, :]), [b_ws], [b_wb])
                        wts_.append((wb, b_wb))
                    return wts_
                _ws = pc.stream([(lambda ct=ct: _ld3(ct)) for ct in range(8)])
                gtr = Ring(fw, "gtr", [128, 512], BF16, 9)

                def _ldg(j, t0, tn):
                    g3 = []
                    for br in range(3):
                        gt, b_gt = gtr.next()
                        DM(gt[:], GT[br * 16 + j, :, t0:t0 + tn], r=[b_scr["GT"]], w=[b_gt])
                        g3.append((gt, b_gt))
                    return g3
                _gs = pc.stream([(lambda j=ct * 2 + hh, t0=t0, tn=tn: _ldg(j, t0, tn)) for ct in range(8) for hh in range(2) for (t0, tn) in TG])
                for ct in range(8):
                    wts = _ws.get()
                    for hh in range(2):
                        j = ct * 2 + hh
                        for (t0, tn) in TG:
                            acc, b_acc = stg.next()
                            g3 = _gs.get()
                            for br in range(3):
                                wb, b_wb = wts[br]
                                xT_, b_xT_, kofs = srcs[br]
                                pt, b_pt = psF.next()
                                mm_F(pt, b_pt, wb, b_wb, 8, hh * 128, xT_, b_xT_, t0, tn, kofs=kofs)
                                gt, b_gt = g3[br]
                                if br == 0:
                                    V(lambda e, acc=acc, pt=pt, gt=gt: e.tensor_tensor(out=acc[:], in0=pt[:], in1=gt[:], op=ALU.mult), [b_pt, b_gt], [b_acc])
                                else:
                                    t2, b_t2 = stg.next()
                                    V(lambda e, t2=t2, pt=pt, gt=gt: e.tensor_tensor(out=t2[:], in0=pt[:], in1=gt[:], op=ALU.mult), [b_pt, b_gt], [b_t2])
                                    if br == 1:
                                        G(lambda e, acc=acc, t2=t2: e.tensor_tensor(out=acc[:], in0=acc[:], in1=t2[:], op=ALU.add), [b_acc, b_t2], [b_acc])
                                    else:
                                        mb, b_mb = stgb.next()
                                        G(lambda e, acc=acc, t2=t2, mb=mb: e.tensor_tensor(out=mb[:], in0=acc[:], in1=t2[:], op=ALU.add), [b_acc, b_t2], [b_mb])
                                        DM(MIXT[j, :, t0:t0 + tn], mb[:], r=[b_mb])
                scr_barrier(["MIXT"])

        def phase_wout(l):
            hT = cur["hT"]
            with PhaseCtx("wout"):
                pc = ProjCtx()
                DM(hT[:], MIXT.rearrange("c p t -> p c t"), r=[b_scr["MIXT"]], w=[b_hT])
                _ws = pc.stream([(lambda ct=ct: pc.load_w(w_out[l, :, ct * 256:(ct + 1) * 256].rearrange("(k p) n -> p k n", p=128), 16, 256))
                                 for ct in range(8)])
                for ct in range(8):
                    wb, b_wb = _ws.get()
                    for tile in range(NT):
                        pt, b_pt = psF.next()
                        mm_T(pt, b_pt, wb, b_wb, 16, 256, hT, b_hT, tile)
                        pc.store_f32(MO[tile * 128:(tile + 1) * 128, ct * 256:(ct + 1) * 256], pt, b_pt, 256)
                all_dma_into(b_X["MO"])

        def phase_ffn(l):
            hT = cur["hT"]
            for qq in range(4):
                with PhaseCtx("ffn%d" % qq):
                    pc = ProjCtx(nstg=3, nstgb=0)
                    gT = fw.sb("gT", [128, 11, T], BF16); b_gT = fw.buf()
                    prevr = Ring(fw, "prevr", [128, 256], F32, 8)
                    def _ldup(hc):
                        ws, b_ws = pc.wst.next()
                        DM(ws[:, :, 0:128], w_up[l, :, hc * 128:(hc + 1) * 128].rearrange("(k p) n -> p k n", p=128), w=[b_ws])
                        ev = fw.dma(lambda e, ws=ws, hc=hc: e.dma_start(
                            out=ws[:, :, 128:256], in_=w_up[l, :, FFN_H + hc * 128:FFN_H + (hc + 1) * 128].rearrange("(k p) n -> p k n", p=128)), [], [])
                        b_ws.w[ev[0]] = ev[1]
                        wb, b_wb = pc.wbf.next()
                        G(lambda e, wb=wb, ws=ws: e.tensor_copy(out=wb[:], in_=ws[:]), [b_ws], [b_wb])
                        return wb, b_wb
                    _ws = pc.stream([(lambda hc=qq * 11 + jj: _ldup(hc)) for jj in range(11)] +
                                    [(lambda ct=ct: pc.load_w(w_dn[l, qq * 1408:(qq + 1) * 1408, ct * 256:(ct + 1) * 256].rearrange("(k p) n -> p k n", p=128), 11, 256))
                                     for ct in range(8)])
                    for jj in range(11):
                        hc = qq * 11 + jj
                        wb, b_wb = _ws.get()
                        for (t0, tn) in TG:
                            pg, b_pg = psF.next()
                            mm_F(pg, b_pg, wb, b_wb, 16, 0, hT, b_hT, t0, tn)
                            pu, b_pu = psF.next()
                            mm_F(pu, b_pu, wb, b_wb, 16, 128, hT, b_hT, t0, tn)
                            sg, b_sg = pc.stg.next()
                            A(lambda e, sg=sg, pg=pg: e.activation(out=sg[:], in_=pg[:], func=AF.Silu), [b_pg], [b_sg])
                            V(lambda e, sg=sg, pu=pu, jj=jj, t0=t0, tn=tn: e.tensor_tensor(out=gT[:, jj, t0:t0 + tn], in0=pu[:], in1=sg[:], op=ALU.mult),
                              [b_pu, b_sg], [b_gT])
                    def _ldpv(ct, tile):
                        pv, b_pv = prevr.next()
                        DM(pv[:], MO[tile * 128:(tile + 1) * 128, ct * 256:(ct + 1) * 256], r=[b_X["MO"]], w=[b_pv])
                        return pv, b_pv
                    _pvl = [(lambda ct=ct, tile=tile: _ldpv(ct, tile)) for ct in range(8) for tile in range(NT)] if qq > 0 else []
                    _pvq = []
                    _pvi = [0]

                    def _pv_fill(depth_):
                        while _pvi[0] < len(_pvl) and len(_pvq) < depth_:
                            _pvq.append(_pvl[_pvi[0]]())
                            _pvi[0] += 1
                    for ct in range(8):
                        wb, b_wb = _ws.get()
                        for tile in range(NT):
                            if qq > 0:
                                _pv_fill(5)
                            pt, b_pt = psF.next()
                            mm_T(pt, b_pt, wb, b_wb, 11, 256, gT, b_gT, tile)
                            dst = MO[tile * 128:(tile + 1) * 128, ct * 256:(ct + 1) * 256]
                            if qq == 0:
                                pc.store_f32(dst, pt, b_pt, 256)
                            else:
                                pv, b_pv = _pvq.pop(0)
                                V(lambda e, pv=pv, pt=pt: e.tensor_tensor(out=pv[:], in0=pt[:, 0:256], in1=pv[:], op=ALU.add), [b_pt, b_pv], [b_pv])
                                DM(dst, pv[:], r=[b_pv])
                    all_dma_into(b_X["MO"])

        def want(name):
            return stop_after is None or True

        gst = ExitStack()
        fw.stack = gst
        cur["hT"] = fw.sb("hT", [128, 16, T], BF16)
        fw.stack = st
        if stop_after != "init":
            phase_cs()
            phase_mods(0)
        if stop_after not in ("init", "mods"):
            norm_pass(0, xin, b_X["xin"], None, None, None, None, None, None, 0, 1, 0, n_w[0])
        done = False
        for l in range(depth if stop_after not in ("init", "mods", "np1") else 0):
            phase_win(l)
            if stop_after == "win":
                done = True
                break
            gst.close()
            phase_na(l)
            phase_gqa(l)
            phase_ssd(l)
            gst = ExitStack()
            fw.stack = gst
            cur["hT"] = fw.sb("hT", [128, 16, T], BF16)
            fw.stack = st
            if stop_after == "mix":
                done = True
                break
            phase_branch(l)
            phase_wout(l)
            x_src, bx_src = (xin, b_X["xin"]) if l == 0 else (XB, b_X["XB"])
            norm_pass(l, x_src, bx_src, MO, b_X["MO"], 2, n_w[1], XA, b_X["XA"], l, 4, 3, n_w[2])
            if stop_after == "np2":
                done = True
                break
            phase_ffn(l)
            last = (l == depth - 1)
            if last:
                norm_pass(l, XA, b_X["XA"], MO, b_X["MO"], 5, n_w[3], y_out, fw.buf("yout"), None, None, None, None)
            else:
                norm_pass(l, XA, b_X["XA"], MO, b_X["MO"], 5, n_w[3], XB, b_X["XB"], l + 1, 1, 0, n_w[0])
        gst.close()
        with PhaseCtx("fin"):
            pass
    return nc


_CACHE = {}


def kernel(**inp):
    f32 = np.float32
    g = lambda k: np.ascontiguousarray(np.asarray(inp[k], dtype=f32))
    if "nc" not in _CACHE:
        _CACHE["nc"] = build_program(DEPTH)
        _CACHE["consts"] = host_consts()
    nc = _CACHE["nc"]
    kc = _CACHE["consts"]
    xp = g("x_prompt"); xs = g("x_sample")
    shared = {
        "ada_w": g("ada_w"), "ada_b": g("ada_b"),
        "norm_mix_pre": g("norm_mix_pre"), "norm_mix_post": g("norm_mix_post"),
        "norm_ffn_pre": g("norm_ffn_pre"), "norm_ffn_post": g("norm_ffn_post"),
        "w_in": g("w_in"), "gate_b": g("gate_b"), "na_rpb": g("na_rpb").reshape(DEPTH, -1),
        "gqa_q_norm": g("gqa_q_norm"), "gqa_k_norm": g("gqa_k_norm"),
        "ssd_conv_w": g("ssd_conv_w"), "ssd_conv_b": g("ssd_conv_b"),
        "ssd_dt_bias": g("ssd_dt_bias").reshape(DEPTH, 32), "ssd_a_log": g("ssd_a_log").reshape(DEPTH, 32),
        "ssd_d": g("ssd_d").reshape(DEPTH, 32), "ssd_norm_w": g("ssd_norm_w"),
        "w_branch": g("w_branch"), "w_out": g("w_out"), "ffn_w_up": g("ffn_w_up"), "ffn_w_down": g("ffn_w_down"),
        "k_ident": kc["ident"], "k_ones": kc["ones"], "k_ssdm": kc["ssdm"], "k_ropec": kc["ropec"], "k_ropes": kc["ropes"],
        "k_ropeR": kc["ropeR"], "k_flipJ": kc["flipJ"], "k_namask": kc["namask"],
    }
    cnk, cnv, cgk, cgv, sst = g("cache_na_k"), g("cache_na_v"), g("cache_gqa_k"), g("cache_gqa_v"), g("state_ssd")
    cc, cctx = g("c"), g("c_ctx")
    in_maps = []
    for c in range(8):
        b = c // 2
        m = dict(shared)
        m["xin"] = np.ascontiguousarray(np.concatenate([xp[2 * c], xp[2 * c + 1], xs[b]], axis=0))
        m["cv"] = np.ascontiguousarray(np.stack([cctx, cc[b]], axis=0))
        m["c_nak"] = cnk[b]; m["c_nav"] = cnv[b]; m["c_gk"] = cgk[b]; m["c_gv"] = cgv[b]
        m["c_ssd"] = np.ascontiguousarray(sst[b].reshape(DEPTH, 2, 1024, 128))
        in_maps.append(m)
    res = run_bass_kernel_spmd(nc, in_maps, core_ids=list(range(8)))
    R = res.results
    y_prompt = np.empty((16, 256, D), f32)
    y_sample = np.empty((4, 2048, D), f32)
    nak = np.empty((16, DEPTH, 256, 8, 128), f32)
    nav = np.empty((16, DEPTH, 256, 8, 128), f32)
    gk = np.empty((16, DEPTH, 256, 2, 128), f32)
    gv = np.empty((16, DEPTH, 256, 2, 128), f32)
    ssd = np.empty((16, DEPTH, 2, 16, 64, 128), f32)
    for c in range(8):
        r = R[c]
        y = np.asarray(r["y_out"])
        for s in range(2):
            bi = 2 * c + s
            y_prompt[bi] = y[s * 256:(s + 1) * 256]
            nak[bi] = np.asarray(r["o_nak"])[:, s * 256:(s + 1) * 256].reshape(DEPTH, 256, 8, 128)
            nav[bi] = np.asarray(r["o_nav"])[:, s * 256:(s + 1) * 256].reshape(DEPTH, 256, 8, 128)
            gk[bi] = np.asarray(r["o_gk"])[:, s * 256:(s + 1) * 256].reshape(DEPTH, 256, 2, 128)
            gv[bi] = np.asarray(r["o_gv"])[:, s * 256:(s + 1) * 256].reshape(DEPTH, 256, 2, 128)
            ssd[bi] = np.asarray(r["o_ssd"])[s].reshape(DEPTH, 2, 16, 64, 128)
        if c % 2 == 0:
            y_sample[c // 2] = y[512:]
    return (y_prompt, y_sample, nak, nav, gk, gv, ssd)
```
